# Optimizing a Trainium2 kernel written in Bass

```python
import math
import jax, jax.numpy as jnp
from jax import lax
import numpy as np

D_MODEL = 2048
BATCH = 1
SEQ = 8192
DEPTH = 1

D_MIX = D_MODEL
HEAD_DIM = 64
D_RWKV = D_MIX // 2
N_RWKV_HEADS = D_RWKV // HEAD_DIM
DECAY_LORA = 64
AAA_LORA = 64
D_ATT = D_MIX - D_RWKV
N_Q_HEADS = D_ATT // HEAD_DIM
N_KV_HEADS = 2
KV_GROUP = N_Q_HEADS // N_KV_HEADS
D_KV = N_KV_HEADS * HEAD_DIM
WINDOW = 128
BLOCK = 128
N_BUCKETS = 32
MAX_EXACT = N_BUCKETS // 2
MAX_DISTANCE = 128
NORM_EPS = 1e-6
LNX_EPS = 64e-5
RWKV_SPLITS = [D_RWKV, 2 * D_RWKV, 3 * D_RWKV, 4 * D_RWKV, 4 * D_RWKV + DECAY_LORA]
RWKV_COLS = 4 * D_RWKV + DECAY_LORA + AAA_LORA
ATT_SPLITS = [D_ATT, D_ATT + D_KV, D_ATT + 2 * D_KV]
ATT_COLS = 2 * D_ATT + 2 * D_KV
D_IN = RWKV_COLS + ATT_COLS

kernel_name = "hymba_rwkv7_swa_sink_hybrid"


def rms_norm(x, g, eps):
    xf = x.astype(jnp.float32)
    y = xf * lax.rsqrt(jnp.mean(xf * xf, axis=-1, keepdims=True) + eps)
    return y * g.astype(jnp.float32)


def rwkv7_mixer(z, mu, w0, w2, a0, a2, k_k, k_a, r_k, lnx_w, lnx_b):
    B_, S_ = z.shape[0], z.shape[1]
    z_prev = jnp.pad(z, ((0, 0), (1, 0), (0, 0)))[:, :-1]
    z = z + (z_prev - z) * mu
    r, k, v, g, wl, al = jnp.split(z, RWKV_SPLITS, axis=-1)
    w = -jax.nn.softplus(-(w0 + jnp.tanh(wl) @ w2)) - 0.5
    decay = jnp.exp(-jnp.exp(w))
    a = jax.nn.sigmoid(a0 + al @ a2)
    hs = lambda t: t.reshape(B_, S_, N_RWKV_HEADS, HEAD_DIM)
    kk = hs(k * k_k)
    kk = kk * lax.rsqrt(jnp.maximum(jnp.sum(kk * kk, axis=-1, keepdims=True), 1e-24))
    k = k * (1.0 + (a - 1.0) * k_a)
    r, k, v, decay, a = hs(r), hs(k), hs(v), hs(decay), hs(a)

    def step(state, inp):
        r_t, w_t, k_t, v_t, kk_t, a_t = inp
        sa = jnp.einsum('bhvk,bhk->bhv', state, -kk_t)
        state = (state * w_t[:, :, None, :]
                 + sa[..., None] * (kk_t * a_t)[:, :, None, :]
                 + v_t[..., None] * k_t[:, :, None, :])
        return state, jnp.einsum('bhvk,bhk->bhv', state, r_t)

    tm = lambda t: jnp.moveaxis(t, 1, 0)
    s0 = jnp.zeros((B_, N_RWKV_HEADS, HEAD_DIM, HEAD_DIM), jnp.float32)
    _, o = lax.scan(step, s0, (tm(r), tm(decay), tm(k), tm(v), tm(kk), tm(a)))
    o = jnp.moveaxis(o, 0, 1)
    mean = jnp.mean(o, axis=-1, keepdims=True)
    var = jnp.mean(jnp.square(o - mean), axis=-1, keepdims=True)
    o = ((o - mean) * lax.rsqrt(var + LNX_EPS)).reshape(B_, S_, D_RWKV) * lnx_w + lnx_b
    bonus = jnp.sum(r * k * r_k, axis=-1, keepdims=True) * v
    o = o + bonus.reshape(B_, S_, D_RWKV)
    return o * jax.nn.silu(g)


def t5_bucket(dist):
    n = jnp.maximum(dist, 0)
    nf = jnp.maximum(n, 1).astype(jnp.float32)
    large = MAX_EXACT + (jnp.log(nf / MAX_EXACT) / math.log(MAX_DISTANCE / MAX_EXACT)
                         * (N_BUCKETS - MAX_EXACT)).astype(jnp.int32)
    large = jnp.minimum(large, N_BUCKETS - 1)
    return jnp.where(n < MAX_EXACT, n, large)


def swa_sink_mixer(z, q_norm_w, k_norm_w, sinks, rel_bias):
    B_, S_ = z.shape[0], z.shape[1]
    nb = S_ // BLOCK
    q, k, v, g = jnp.split(z, ATT_SPLITS, axis=-1)
    q = rms_norm(q.reshape(B_, S_, N_Q_HEADS, HEAD_DIM), q_norm_w, NORM_EPS)
    k = rms_norm(k.reshape(B_, S_, N_KV_HEADS, HEAD_DIM), k_norm_w, NORM_EPS)
    v = v.reshape(B_, S_, N_KV_HEADS, HEAD_DIM)
    qb = q.reshape(B_, nb, BLOCK, N_KV_HEADS, KV_GROUP, HEAD_DIM)

    def band(t):
        tb = t.reshape(B_, nb, BLOCK, N_KV_HEADS, HEAD_DIM)
        prev = jnp.pad(tb, ((0, 0), (1, 0), (0, 0), (0, 0), (0, 0)))[:, :-1]
        return jnp.concatenate([prev, tb], axis=2)

    kb, vb = band(k), band(v)
    logits = jnp.einsum('bnqhgd,bnkhd->bnhgqk', qb, kb) * (HEAD_DIM ** -0.5)
    qi = jnp.arange(BLOCK)[:, None]
    kj = jnp.arange(2 * BLOCK)[None, :]
    dist = BLOCK + qi - kj
    bias = rel_bias.astype(jnp.float32)[t5_bucket(dist)]
    bias = jnp.transpose(bias, (2, 0, 1)).reshape(N_KV_HEADS, KV_GROUP, BLOCK, 2 * BLOCK)
    kpos = (jnp.arange(nb)[:, None, None] - 1) * BLOCK + kj[None]
    mask = (dist >= 0) & (dist < WINDOW) & (kpos >= 0)
    logits = jnp.where(mask[None, :, None, None], logits + bias, -jnp.inf)
    sink = sinks.astype(jnp.float32).reshape(1, 1, N_KV_HEADS, KV_GROUP, 1, 1)
    m = jnp.maximum(jnp.max(logits, axis=-1, keepdims=True), sink)
    e = jnp.exp(logits - m)
    p = e / (jnp.sum(e, axis=-1, keepdims=True) + jnp.exp(sink - m))
    o = jnp.einsum('bnhgqk,bnkhd->bnqhgd', p, vb).reshape(B_, S_, D_ATT)
    return o * jax.nn.silu(g)


def setup_inputs(seed: int = 0) -> dict:
    key = jax.random.key(seed)
    ks = jax.random.split(key, 20)
    f32 = jnp.float32
    nrm = lambda k, shape, s: jax.random.normal(k, shape, f32) * s
    L = DEPTH
    return {
        "x": jax.random.normal(ks[0], (BATCH, SEQ, D_MODEL), f32),
        "norm_w": 1.0 + nrm(ks[1], (L, D_MODEL), 0.02),
        "w_in": nrm(ks[2], (L, D_MODEL, D_IN), D_MODEL ** -0.5),
        "w_out": nrm(ks[3], (L, D_MIX, D_MODEL), D_MIX ** -0.5),
        "mu_rwkv": jax.random.uniform(ks[4], (L, RWKV_COLS), f32, 0.0, 1.0),
        "w0": jax.random.uniform(ks[5], (L, D_RWKV), f32, -6.0, 0.0),
        "w2": nrm(ks[6], (L, DECAY_LORA, D_RWKV), 0.1),
        "a0": nrm(ks[7], (L, D_RWKV), 0.5),
        "a2": nrm(ks[8], (L, AAA_LORA, D_RWKV), 0.5 * AAA_LORA ** -0.5),
        "k_k": 0.85 + nrm(ks[9], (L, D_RWKV), 0.05),
        "k_a": 1.0 + nrm(ks[10], (L, D_RWKV), 0.05),
        "r_k": nrm(ks[11], (L, N_RWKV_HEADS, HEAD_DIM), 0.1),
        "lnx_w": 1.0 + nrm(ks[12], (L, D_RWKV), 0.02),
        "lnx_b": nrm(ks[13], (L, D_RWKV), 0.02),
        "q_norm_w": 1.0 + nrm(ks[14], (L, HEAD_DIM), 0.02),
        "k_norm_w": 1.0 + nrm(ks[15], (L, HEAD_DIM), 0.02),
        "sinks": nrm(ks[16], (L, N_Q_HEADS), 0.5),
        "rel_bias": nrm(ks[17], (N_BUCKETS, N_Q_HEADS), 0.2),
    }


def reference(x, norm_w, w_in, w_out, mu_rwkv, w0, w2, a0, a2, k_k, k_a, r_k,
              lnx_w, lnx_b, q_norm_w, k_norm_w, sinks, rel_bias):
    for l in range(DEPTH):
        h = rms_norm(x, norm_w[l], NORM_EPS).astype(x.dtype)
        z = (h @ w_in[l]).astype(jnp.float32)
        y_rwkv = rwkv7_mixer(z[..., :RWKV_COLS], mu_rwkv[l], w0[l], w2[l], a0[l], a2[l],
                             k_k[l], k_a[l], r_k[l], lnx_w[l], lnx_b[l])
        y_att = swa_sink_mixer(z[..., RWKV_COLS:], q_norm_w[l], k_norm_w[l], sinks[l], rel_bias)
        y = jnp.concatenate([y_rwkv, y_att], axis=-1).astype(x.dtype)
        x = x + y @ w_out[l]
    return x
```

```python
import math
import numpy as np
import concourse.bass as bass
import concourse.mybir as mybir
from concourse.bass_utils import run_bass_kernel_spmd
from contextlib import ExitStack

F32 = mybir.dt.float32
BF16 = mybir.dt.bfloat16
ALU = mybir.AluOpType
AF = mybir.ActivationFunctionType

NCORE = 8
D = 2048
KC = D // 128
TT_ = 512
SAME_ENGINE_SYNC = True
NDS = 24
C0 = math.exp(-0.5)
NORM_EPS = 1e-6
import os
STAGE = int(os.environ.get('KSTAGE', '99'))
SUB = int(os.environ.get('KSUB', '99'))
GS = int(os.environ.get('KGS', '99'))
KM = int(os.environ.get('KM', '31'))
KNS = int(os.environ.get('KNS', '6'))
KNE = int(os.environ.get('KNE', '1'))
LNX_EPS = 64e-5


class Buf:
    __slots__ = ("w", "r")

    def __init__(self):
        self.w = None
        self.r = []


class Prog:
    ENG = ["tensor", "vector", "scalar", "gpsimd", "sync"]

    def __init__(self, nc, es):
        self.nc = nc
        self.ops = {e: [] for e in self.ENG}
        self.cnt = {e: 0 for e in self.ENG}
        self.sem = {}
        for e in self.ENG:
            if e != "sync":
                self.sem[e] = es.enter_context(nc.semaphore("s_" + e))
        for i in range(NDS):
            self.sem["d%d" % i] = es.enter_context(nc.semaphore("s_d%d" % i))
        self.seen = {e: {} for e in self.ENG}
        self.ndma = 0

    def _deps(self, eng, reads, writes):
        waits = {}

        def need(t):
            if t is None:
                return
            e, v = t
            if e == eng:
                if eng == "tensor" or not SAME_ENGINE_SYNC or v > self.cnt[eng]:
                    return
            if waits.get(e, 0) < v:
                waits[e] = v

        for b in reads:
            need(b.w)
        for b in writes:
            need(b.w)
            for t in b.r:
                need(t)
        w2 = []
        for e, v in waits.items():
            if self.seen[eng].get(e, 0) < v:
                self.seen[eng][e] = v
                w2.append((e, v))
        return w2

    def op(self, eng, fn, reads=(), writes=(), sig=True):
        reads = [getattr(b, "b", b) for b in reads]
        writes = [getattr(b, "b", b) for b in writes]
        w2 = self._deps(eng, reads, writes)
        ticket = (eng, self.cnt[eng] + 1)
        if sig:
            self.cnt[eng] += 1
        for b in reads:
            b.r.append(ticket)
        for b in writes:
            b.w = ticket
            b.r = []
        self.ops[eng].append((fn, w2, (eng, 1) if sig else None))

    def dma(self, fn, reads=(), writes=(), eng="sync"):
        reads = [getattr(b, "b", b) for b in reads]
        writes = [getattr(b, "b", b) for b in writes]
        w2 = self._deps(eng, reads, writes)
        k = self.ndma % NDS
        v = 16 * (self.ndma // NDS + 1)
        self.ndma += 1
        ticket = ("d%d" % k, v)
        for b in reads:
            b.r.append(ticket)
        for b in writes:
            b.w = ticket
            b.r = []
        self.ops[eng].append((fn, w2, ("d%d" % k, 16)))
        return ticket

    def wait_all(self, eng, tickets):
        w2 = []
        for e, v in tickets:
            if self.seen[eng].get(e, 0) < v:
                self.seen[eng][e] = v
                w2.append((e, v))
        self.ops[eng].append((None, w2, None))

    def emit(self):
        nc = self.nc
        with nc.Block() as block:
            def mk(ename):
                def body(eng):
                    for fn, waits, sig in self.ops[ename]:
                        for e, v in waits:
                            eng.wait_ge(self.sem[e], v)
                        if fn is None:
                            continue
                        ins = fn(eng)
                        if sig is not None:
                            ins.then_inc(self.sem[sig[0]], sig[1])
                return body
            block.tensor(mk("tensor"))
            block.vector(mk("vector"))
            block.scalar(mk("scalar"))
            block.gpsimd(mk("gpsimd"))
            block.sync(mk("sync"))


class T:
    def __init__(self, t):
        self.t = t
        self.b = Buf()

    def __getitem__(self, k):
        return self.t[k]


def build(S, fused=True):
    NT = S // TT_
    nc = bass.Bass("TRN2", target_bir_lowering=False)
    din = lambda n, sh, dt=F32: nc.dram_tensor(n, sh, dt, kind="ExternalInput").ap()
    xT = din("xT", [D, S])
    wc = din("wc", [D, 1024])
    if fused:
        xres = din("xres", [256, S])
        woc = din("woc", [D, 256])
    w2a2 = din("w2a2", [128, 128])
    pp = din("pp", [128, 32])
    cst = din("cst", [128, 1728])
    abias = din("abias", [128, 512])
    if fused:
        outT = nc.dram_tensor("outT", [256, S], F32, kind="ExternalOutput").ap()
        a_in = nc.dram_tensor("a_in", [256, S], BF16)
    else:
        a_in = nc.dram_tensor("yc", [256, S], BF16, kind="ExternalOutput")
    DBG = bool(int(os.environ.get('KDBG', '0')))
    if DBG:
        dbg_y = nc.dram_tensor("dbg_y", [256, S], BF16, kind="ExternalOutput").ap()
    if fused:
        a_out = nc.dram_tensor("a_out", [NCORE * 256, S], BF16)

    es = ExitStack()
    with es:
        P = Prog(nc, es)
        cnt = [0]

        def sb(shape, dt=F32):
            cnt[0] += 1
            return T(es.enter_context(nc.sbuf_tensor("sb%d" % cnt[0], shape, dt)))

        def psb(shape, dt=F32):
            cnt[0] += 1
            return T(es.enter_context(nc.psum_tensor("ps%d" % cnt[0], shape, dt)))

        def tt(eng, out, in0, in1, op, R, W):
            P.op(eng, lambda e: e.tensor_tensor(out=out, in0=in0, in1=in1, op=op), R, W)

        def stt(eng, out, in0, scalar, in1, op0, op1, R, W):
            P.op(eng, lambda e: e.scalar_tensor_tensor(out=out, in0=in0, scalar=scalar, in1=in1, op0=op0, op1=op1), R, W)

        def ts(eng, out, in0, s1, s2, op0, op1, R, W):
            if s2 is None:
                P.op(eng, lambda e: e.tensor_scalar(out=out, in0=in0, scalar1=s1, scalar2=None, op0=op0), R, W)
            else:
                P.op(eng, lambda e: e.tensor_scalar(out=out, in0=in0, scalar1=s1, scalar2=s2, op0=op0, op1=op1), R, W)

        def cp(eng, out, in_, R, W):
            if eng == "scalar":
                P.op(eng, lambda e: e.copy(out=out, in_=in_), R, W)
            else:
                P.op(eng, lambda e: e.tensor_copy(out=out, in_=in_), R, W)

        def act(out, in_, func, R, W, bias=None, scale=None):
            kw = {}
            if bias is not None:
                kw["bias"] = bias
            if scale is not None:
                kw["scale"] = scale
            P.op("scalar", lambda e: e.activation(out=out, in_=in_, func=func, **kw), R, W)

        def mm(out, lhsT, rhs, R, W, start=True, stop=True, sig=True):
            P.op("tensor", lambda e: e.matmul(out, lhsT=lhsT, rhs=rhs, start=start, stop=stop), R, W, sig=sig)

        def recip(eng, out, in_, R, W):
            P.op(eng, lambda e: e.reciprocal(out=out, in_=in_), R, W)

        cst_t = sb([128, 1728])
        pp_t = sb([128, 32])
        dp = sb([128, 8])
        w2a2_t = sb([128, 128])
        ab_t = sb([128, 512])
        EB = sb([128, 512], BF16)
        identb = sb([128, 128], BF16)
        onesbd_b = sb([128, 128], BF16)
        ones_b = sb([128, 128], BF16)
        Wb = sb([128, KC, 1024], BF16)
        Wo = sb([128, KC, 256], BF16)
        P.dma(lambda e: e.dma_start(out=cst_t[:], in_=cst[:, :]), writes=[cst_t])
        P.dma(lambda e: e.dma_start(out=pp_t[:], in_=pp[:, :]), writes=[pp_t])
        P.dma(lambda e: e.dma_start(out=w2a2_t[:], in_=w2a2[:, :]), writes=[w2a2_t])
        P.dma(lambda e: e.dma_start(out=ab_t[:], in_=abias[:, :]), writes=[ab_t])
        ident_f = cst_t[:, 0:128]
        onesbd_f = cst_t[:, 128:256]
        maskA = cst_t[:, 256:640]
        maskB = cst_t[:, 640:896]
        i2_f = cst_t[:, 896:960]
        resetm = cst_t[:, 960:1472]
        cp("vector", identb[:], ident_f, [cst_t], [identb])
        cp("vector", onesbd_b[:], onesbd_f, [cst_t], [onesbd_b])
        P.op("vector", lambda e: e.memset(ones_b[:], 1.0), [], [ones_b])
        ts("vector", dp[:, 0:1], pp_t[:, 8:9], -1.0, 1.0, ALU.mult, ALU.add, [pp_t], [dp])
        ts("vector", dp[:, 1:2], pp_t[:, 12:13], 0.125, None, ALU.mult, None, [pp_t], [dp])
        act(dp[:, 2:3], pp_t[:, 14:15], AF.Exp, [pp_t], [dp])
        act(dp[:, 5:6], pp_t[:, 15:16], AF.Exp, [pp_t], [dp])
        P.op("vector", lambda e: e.memset(dp[:, 3:4], NORM_EPS), [], [dp])
        P.op("vector", lambda e: e.memset(dp[:, 4:5], LNX_EPS), [], [dp])
        ebf = sb([128, 512])
        act(ebf[:], ab_t[:], AF.Exp, [ab_t], [ebf])
        for h in range(2):
            tt("vector", EB[:, h * 256:(h + 1) * 256], ebf[:, h * 256:(h + 1) * 256], cst_t[:, 1472:1728], ALU.mult, [ebf, cst_t], [EB])
        wst = [sb([128, 1024]) for _ in range(2)]
        for kc in range(KC):
            s_ = wst[kc % 2]
            P.dma(lambda e, kc=kc, s_=s_: e.dma_start(out=s_[:], in_=wc[kc * 128:(kc + 1) * 128, :]), writes=[s_])
            eng = "vector" if kc % 2 == 0 else "gpsimd"
            ts(eng, Wb[:, kc, :], s_[:], pp_t[:, 16 + kc:17 + kc], None, ALU.mult, None, [s_, pp_t], [Wb])
        for kc in range(KC if fused else 0):
            s_ = wst[kc % 2]
            P.dma(lambda e, kc=kc, s_=s_: e.dma_start(out=s_[:, 0:256], in_=woc[kc * 128:(kc + 1) * 128, :]), writes=[s_])
            cp("vector" if kc % 2 == 0 else "gpsimd", Wo[:, kc, :], s_[:, 0:256], [s_], [Wo])

        zs = [sb([128, TT_ + 1]) for _ in range(8)]
        for j in range(5):
            P.op("gpsimd", lambda e, j=j: e.memset(zs[j][:, 0:1], 0.0), [], [zs[j]])
        Sm = sb([128, 64])
        Sst = sb([128, 64], BF16)
        P.op("vector", lambda e: e.memset(Sm[:], 0.0), [], [Sm])
        P.op("vector", lambda e: e.memset(Sst[:], 0.0), [], [Sst])
        NSL = 8
        KN = [sb([128, 128], BF16) for _ in range(NSL)]
        VA = [sb([128, 128], BF16) for _ in range(NSL)]
        VB = [sb([128, 128], BF16) for _ in range(NSL)]
        for s_ in range(NSL):
            P.op("gpsimd", lambda e, s_=s_: e.memset(VA[s_][:, 64:128], 1.0), [], [VA[s_]])
            P.op("gpsimd", lambda e, s_=s_: e.memset(VB[s_][:, 0:64], 1.0), [], [VB[s_]])

        xst = [sb([128, TT_]) for _ in range(4)]
        xsq = [sb([128, TT_], BF16) for _ in range(4)]
        xb = sb([128, KC, TT_], BF16)
        rstd = sb([128, TT_])
        zsh = [sb([128, TT_]) for _ in range(5)]
        tmp = [sb([128, TT_]) for _ in range(14)]
        tb = [sb([128, TT_], BF16) for _ in range(4)]
        AR = sb([128, 8, 2, 64], BF16)
        YT = sb([128, 8, 2, 64], BF16)
        YH = sb([128, 8, 2, 64], BF16)
        gC = sb([128, 8])
        vb = sb([128, TT_], BF16)
        SA = sb([128, 2, 192], BF16)
        SBs = sb([128, 2, 128], BF16)
        NLT = [sb([128, 2, 192], BF16) for _ in range(2)]
        TRs = sb([128, 2, 192], BF16)
        Ybf = sb([128, 64], BF16)
        Ubf = sb([128, 64], BF16)
        Ot = sb([128, TT_])
        qn = sb([128, TT_], BF16)
        Es = sb([128, 512], BF16)
        Em = sb([128, 512], BF16)
        yT = [sb([128, TT_], BF16) for _ in range(2)]
        ya = sb([128, KC, TT_], BF16)
        xr = [sb([128, TT_]) for _ in range(2)]
        ost = [sb([128, TT_]) for _ in range(2)]

        ps_in = [psb([128, 512]) for _ in range(2)]
        ps_st = psb([128, 512])
        ps_sc = [psb([128, 512]) for _ in range(2)]
        ps_ch = psb([128, 512])
        ps_o = psb([128, 512])
        ps_at = psb([128, 512])

        out_tickets = []
        bin_ = Buf()
        bout = Buf()

        for i in range(NT if STAGE >= 1 else 0):
            t0 = i * TT_
            for kc in range(KC):
                s_ = xst[kc % 4]
                q_ = xsq[kc % 4]
                P.dma(lambda e, kc=kc, s_=s_, t0=t0: e.dma_start(out=s_[:], in_=xT[kc * 128:(kc + 1) * 128, t0:t0 + TT_]), writes=[s_])
                act(q_[:], s_[:], AF.Square, [s_], [q_])
                cp("vector" if kc % 2 == 0 else "gpsimd", xb[:, kc, :], s_[:], [s_], [xb])
                mm(ps_st[:, :], ones_b[:], q_[:], [ones_b, q_], [ps_st], start=(kc == 0), stop=(kc == KC - 1), sig=True)
            act(rstd[:], ps_st[:, :], AF.Sqrt, [ps_st, dp], [rstd], bias=dp[:, 3:4], scale=1.0 / D)
            recip("vector", rstd[:], rstd[:], [rstd], [rstd])
            if STAGE < 2:
                continue
            for j in range(8):
                pacc = ps_in[j % 2]
                for kc in range(KC):
                    mm(pacc[:, :], Wb[:, kc, j * 128:(j + 1) * 128], xb[:, kc, :], [Wb, xb], [pacc], start=(kc == 0), stop=(kc == KC - 1), sig=(kc == KC - 1))
                tt("vector", zs[j][:, 1:TT_ + 1], pacc[:, :], rstd[:], ALU.mult, [pacc, rstd], [zs[j]])
            if STAGE < 3:
                continue
            for j in range(5):
                d_ = tmp[0]
                tt("gpsimd", d_[:], zs[j][:, 0:TT_], zs[j][:, 1:TT_ + 1], ALU.subtract, [zs[j]], [d_])
                stt("vector", zsh[j][:], d_[:], pp_t[:, j:j + 1], zs[j][:, 1:TT_ + 1], ALU.mult, ALU.add, [d_, zs[j], pp_t], [zsh[j]])
                cp("scalar", zs[j][:, 0:1], zs[j][:, TT_:TT_ + 1], [zs[j]], [zs[j]])
            r_, k_, v_, g_, lo_ = zsh
            tl = tmp[0]
            act(tl[0:64, :], lo_[0:64, :], AF.Tanh, [lo_], [tl])
            mm(ps_in[0][:, :], w2a2_t[0:64, :], tl[0:64, :], [w2a2_t, tl], [ps_in[0]])
            mm(ps_in[1][:, :], w2a2_t[64:128, :], lo_[64:128, :], [w2a2_t, lo_], [ps_in[1]])
            sg = tmp[1]
            a_ = tmp[2]
            act(sg[:], ps_in[0][:, :], AF.Sigmoid, [ps_in[0], pp_t], [sg], bias=pp_t[:, 5:6])
            act(a_[:], ps_in[1][:, :], AF.Sigmoid, [ps_in[1], pp_t], [a_], bias=pp_t[:, 6:7])
            cum = tmp[3]
            P.op("vector", lambda e: e.tensor_tensor_scan(out=cum[:], data0=resetm, data1=sg[:], initial=0.0, op0=ALU.mult, op1=ALU.add), [sg, cst_t], [cum])
            cumx = tmp[4]
            tt("gpsimd", cumx[:], cum[:], sg[:], ALU.subtract, [cum, sg], [cumx])
            E1 = tmp[5]
            E2 = tmp[6]
            E3 = tmp[7]
            Eh = tmp[8]
            act(E1[:], cum[:], AF.Exp, [cum], [E1], scale=-C0)
            act(E2[:], cum[:], AF.Exp, [cum], [E2], scale=C0)
            act(E3[:], cumx[:], AF.Exp, [cumx], [E3], scale=-C0)
            cum3 = cum[:].rearrange("p (c t) -> p c t", t=64)
            d2 = cumx
            tt("vector", d2[:].rearrange("p (c t) -> p c t", t=64), cum3, cum3[:, :, 63:64].to_broadcast([128, 8, 64]), ALU.subtract, [cum, cumx], [cumx])
            act(Eh[:], d2[:], AF.Exp, [cumx], [Eh], scale=C0)
            act(gC[:], cum3[:, :, 63], AF.Exp, [cum], [gC], scale=-C0)
            act(tb[0][:], k_[:], AF.Square, [k_, pp_t], [tb[0]], scale=pp_t[:, 7:8])
            mm(ps_st[:, :], onesbd_b[:], tb[0][:], [onesbd_b, tb[0]], [ps_st])
            rs = tmp[9]
            act(rs[:], ps_st[:, :], AF.Sqrt, [ps_st], [rs])
            ts("vector", rs[:], rs[:], 1e-12, None, ALU.max, None, [rs], [rs])
            recip("vector", rs[:], rs[:], [rs], [rs])
            kk = tmp[10]
            stt("vector", kk[:], k_[:], pp_t[:, 7:8], rs[:], ALU.mult, ALU.mult, [k_, pp_t, rs], [kk])
            ka = tmp[11]
            tt("gpsimd", ka[:], kk[:], a_[:], ALU.mult, [kk, a_], [ka])
            AR4 = AR[:].rearrange("p c a t -> p c (a t)")
            def half(Tl, a):
                return Tl[:, :, a, :]
            v3 = lambda tl_: tl_[:].rearrange("p (c t) -> p c t", t=64)
            stt("vector", half(AR, 0), v3(kk), -1.0, v3(E3), ALU.mult, ALU.mult, [kk, E3], [AR])
            tt("gpsimd", half(AR, 1), v3(r_), v3(E1), ALU.mult, [r_, E1], [AR])
            tt("vector", half(YT, 0), v3(ka), v3(E2), ALU.mult, [ka, E2], [YT])
            tt("gpsimd", half(YH, 0), v3(ka), v3(Eh), ALU.mult, [ka, Eh], [YH])
            kp = tmp[9]
            ts("vector", kp[:], a_[:], pp_t[:, 8:9], dp[:, 0:1], ALU.mult, ALU.add, [a_, pp_t, dp, rs], [kp])
            tt("vector", kp[:], kp[:], k_[:], ALU.mult, [kp, k_], [kp])
            tt("vector", half(YT, 1), v3(kp), v3(E2), ALU.mult, [kp, E2], [YT])
            tt("gpsimd", half(YH, 1), v3(kp), v3(Eh), ALU.mult, [kp, Eh], [YH])
            stt("vector", tb[1][:], r_[:], pp_t[:, 9:10], kp[:], ALU.mult, ALU.mult, [r_, pp_t, kp], [tb[1]])
            mm(ps_st[:, :], onesbd_b[:], tb[1][:], [onesbd_b, tb[1]], [ps_st])
            bon = tmp[12]
            tt("vector", bon[:], ps_st[:, :], v_[:], ALU.mult, [ps_st, v_], [bon])
            cp("gpsimd", vb[:], v_[:], [v_], [vb])
            sil = tmp[13]
            act(sil[:], g_[:], AF.Silu, [g_], [sil])

            if STAGE < 4:
                continue
            for m in range(4):
                cs = [2 * m, 2 * m + 1]
                pa, pb = ps_sc
                for cc, c in enumerate(cs):
                    for h in range(2):
                        hs = slice(h * 64, (h + 1) * 64)
                        last = (cc == 1 and h == 1)
                        mm(pa[hs, cc * 192:cc * 192 + 64], AR[hs, c, 0, :], YT[hs, c, 0, :], [AR, YT], [pa], sig=False)
                        mm(pa[hs, cc * 192 + 64:cc * 192 + 192], YT[hs, c, 0, :], AR[hs, c, :, :].rearrange("p a t -> p (a t)"), [AR, YT], [pa], sig=False)
                        mm(pb[hs, cc * 128:cc * 128 + 128], YT[hs, c, 1, :], AR[hs, c, :, :].rearrange("p a t -> p (a t)"), [AR, YT], [pb], sig=last)
                tt("vector", SA[:].rearrange("p c x -> p (c x)"), pa[:, 0:384], maskA, ALU.mult, [pa, cst_t], [SA])
                tt("vector", SBs[:].rearrange("p c x -> p (c x)"), pb[:, 0:256], maskB, ALU.mult, [pb, cst_t], [SBs])
                if SUB < 2:
                    continue
                for cc, c in enumerate(cs):
                    for h in range(2):
                        hs = slice(h * 64, (h + 1) * 64)
                        last = (cc == 1 and h == 1)
                        idh = identb[hs, h * 64:(h + 1) * 64]
                        mm(pb[hs, 256 + cc * 64:256 + cc * 64 + 64], vb[hs, c * 64:(c + 1) * 64], idh, [vb, identb], [pb], sig=False)
                        mm(pa[hs, 384 + cc * 64:384 + cc * 64 + 64], YH[hs, c, 0, :], idh, [YH, identb], [pa], sig=False)
                        mm(pb[hs, 384 + cc * 64:384 + cc * 64 + 64], YH[hs, c, 1, :], idh, [YH, identb], [pb], sig=last)
                for cc in range(2):
                    cp("scalar", TRs[:, cc, 0:64], pb[:, 256 + cc * 64:256 + cc * 64 + 64], [pb], [TRs])
                    cp("scalar", TRs[:, cc, 64:128], pa[:, 384 + cc * 64:384 + cc * 64 + 64], [pa], [TRs])
                    cp("scalar", TRs[:, cc, 128:192], pb[:, 384 + cc * 64:384 + cc * 64 + 64], [pb], [TRs])
                if SUB < 3:
                    continue
                cur = 0
                cp("vector", NLT[0][:, :, 0:128], SA[:, :, 0:128], [SA], [NLT[0]])
                tt("vector", NLT[0][:, :, 128:192], SA[:, :, 64:128], i2_f.unsqueeze(1).to_broadcast([128, 2, 64]), ALU.add, [SA, cst_t], [NLT[0]])
                for stg in range(1, KNS + 1):
                    pn = ps_ch
                    src = NLT[cur]
                    need_sq = stg <= 5
                    need_lt = stg <= 4
                    need_t = stg >= 2
                    for cc in range(2):
                        for h in range(2):
                            hs = slice(h * 64, (h + 1) * 64)
                            last = (cc == 1 and h == 1)
                            Lc = src[hs, cc, 0:64]
                            LTc = src[hs, cc, 64:128]
                            Tc = src[hs, cc, 128:192]
                            idh = identb[hs, h * 64:(h + 1) * 64]
                            if need_sq:
                                mm(pn[hs, cc * 192:cc * 192 + 64], LTc, Lc, [src], [pn], sig=False)
                            if need_lt:
                                mm(pn[hs, cc * 192 + 64:cc * 192 + 128], Lc, LTc, [src], [pn], sig=False)
                            mm(pn[hs, cc * 192 + 128:cc * 192 + 192], idh, Tc, [src, identb], [pn], start=True, stop=(not need_t), sig=(last and not need_t))
                            if need_t:
                                mm(pn[hs, cc * 192 + 128:cc * 192 + 192], Lc, Tc, [src], [pn], start=False, stop=True, sig=last)
                    nxt = 1 - cur
                    lo = 0 if need_sq else 128
                    pv = ps_ch[:, 0:384].rearrange("p (c x) -> p c x", x=192)[:, :, lo:192]
                    if stg % 2 == 0:
                        P.op("scalar", lambda e, nxt=nxt, lo=lo, pv=pv: e.copy(out=NLT[nxt][:, :, lo:192], in_=pv), [pn], [NLT[nxt]])
                    else:
                        P.op("vector", lambda e, nxt=nxt, lo=lo, pv=pv: e.tensor_copy(out=NLT[nxt][:, :, lo:192], in_=pv), [pn], [NLT[nxt]])
                    cur = nxt
                TTf = NLT[cur]
                if SUB < 4:
                    continue
                for cc, c in enumerate(cs):
                    pc_ = ps_ch
                    for h in range(2):
                        hs = slice(h * 64, (h + 1) * 64)
                        mm(pc_[hs, 384:448], AR[hs, c, 0, :], Sst[hs, :], [AR, Sst], [pc_], start=True, stop=False, sig=False)
                        mm(pc_[hs, 384:448], SBs[hs, cc, 0:64], TRs[hs, cc, 0:64], [SBs, TRs], [pc_], start=False, stop=True, sig=(h == 1))
                    cp("scalar", Ybf[:], pc_[:, 384:448], [pc_], [Ybf])
                    for h in range(2):
                        hs = slice(h * 64, (h + 1) * 64)
                        mm(pc_[hs, 448:512], TTf[hs, cc, 128:192], Ybf[hs, :], [TTf, Ybf], [pc_], sig=(h == 1))
                    cp("vector", Ubf[:], pc_[:, 448:512], [pc_], [Ubf])
                    for h in range(2):
                        hs = slice(h * 64, (h + 1) * 64)
                        oo = ps_o[hs, c * 64:(c + 1) * 64]
                        mm(oo, Sst[hs, :], AR[hs, c, 1, :], [Sst, AR], [ps_o], start=True, stop=False, sig=False)
                        mm(oo, Ubf[hs, :], SA[hs, cc, 128:192], [Ubf, SA], [ps_o], start=False, stop=False, sig=False)
                        mm(oo, TRs[hs, cc, 0:64], SBs[hs, cc, 64:128], [TRs, SBs], [ps_o], start=False, stop=True, sig=False)
                        mm(pc_[hs, 384:448], TRs[hs, cc, 64:128], Ubf[hs, :], [TRs, Ubf], [pc_], start=True, stop=False, sig=False)
                        mm(pc_[hs, 384:448], TRs[hs, cc, 128:192], TRs[hs, cc, 0:64], [TRs], [pc_], start=False, stop=True, sig=(h == 1))
                    stt("vector", Sm[:], Sm[:], gC[:, c:c + 1], pc_[:, 384:448], ALU.mult, ALU.add, [Sm, gC, pc_], [Sm])
                    cp("vector", Sst[:], Sm[:], [Sm], [Sst])
            if STAGE < 5:
                continue
            cp("scalar", Ot[:], ps_o[:, :], [ps_o], [Ot])
            mm(ps_st[:, :], onesbd_f, Ot[:], [cst_t, Ot], [ps_st])
            xc = tmp[0]
            stt("vector", xc[:], ps_st[:, :], -1.0 / 64, Ot[:], ALU.mult, ALU.add, [ps_st, Ot], [xc])
            xq = tmp[1]
            act(xq[:], xc[:], AF.Square, [xc], [xq])
            mm(ps_st[:, :], onesbd_f, xq[:], [cst_t, xq], [ps_st])
            rv = tmp[2]
            act(rv[:], ps_st[:, :], AF.Sqrt, [ps_st, dp], [rv], bias=dp[:, 4:5], scale=1.0 / 64)
            recip("vector", rv[:], rv[:], [rv], [rv])
            tt("vector", xc[:], xc[:], rv[:], ALU.mult, [xc, rv], [xc])
            ts("vector", xc[:], xc[:], pp_t[:, 10:11], pp_t[:, 11:12], ALU.mult, ALU.add, [xc, pp_t], [xc])
            tt("gpsimd", xc[:], xc[:], bon[:], ALU.add, [xc, bon], [xc])
            tt("vector", yT[0][:], xc[:], sil[:], ALU.mult, [xc, sil], [yT[0]])

            if STAGE < 6:
                continue
            q_ = zs[5]
            kv_ = zs[6]
            ga_ = zs[7]
            act(tb[2][:], q_[:, 1:TT_ + 1], AF.Square, [q_], [tb[2]])
            mm(ps_st[:, :], onesbd_b[:], tb[2][:], [onesbd_b, tb[2]], [ps_st])
            rq = tmp[3]
            act(rq[:], ps_st[:, :], AF.Sqrt, [ps_st, dp], [rq], bias=dp[:, 3:4], scale=1.0 / 64)
            recip("vector", rq[:], rq[:], [rq], [rq])
            stt("vector", qn[:], q_[:, 1:TT_ + 1], dp[:, 1:2], rq[:], ALU.mult, ALU.mult, [q_, dp, rq], [qn])
            act(tb[3][0:64, :], kv_[0:64, 1:TT_ + 1], AF.Square, [kv_], [tb[3]])
            mm(ps_st[0:64, :], onesbd_b[0:64, 0:64], tb[3][0:64, :], [onesbd_b, tb[3]], [ps_st])
            rk = tmp[4]
            act(rk[0:64, :], ps_st[0:64, :], AF.Sqrt, [ps_st, dp], [rk], bias=dp[0:64, 3:4], scale=1.0 / 64)
            recip("vector", rk[0:64, :], rk[0:64, :], [rk], [rk])
            cp("gpsimd", tb[3][:, :], kv_[:, 1:TT_ + 1], [kv_, tb[3]], [tb[3]])
            if GS < 2:
                continue
            for n in range(4):
                gb = 4 * i + n
                sl = gb % NSL
                ts_ = slice(n * 128, (n + 1) * 128)
                if True:
                    stt("vector", KN[sl][0:64, :], kv_[0:64, 1 + n * 128:1 + (n + 1) * 128], pp_t[0:64, 13:14], rk[0:64, ts_], ALU.mult, ALU.mult, [kv_, pp_t, rk], [KN[sl]])
                if True:
                    cp("vector", KN[sl][64:128, :], KN[sl][0:64, :], [KN[sl]], [KN[sl]])
                if True:
                    mm(ps_at[:, 0:64], tb[3][64:128, ts_], identb[64:128, 64:128], [tb[3], identb], [ps_at])
                if True:
                    cp("scalar", VA[sl][:, 0:64], ps_at[:, 0:64], [ps_at], [VA[sl]])
                if True:
                    cp("gpsimd", VB[sl][:, 64:128], VA[sl][:, 0:64], [VA[sl]], [VB[sl]])
            if GS < 3:
                continue
            for n in range(4):
                gb = 4 * i + n
                sl = gb % NSL
                slp = (gb - 1) % NSL
                ts_ = slice(n * 128, (n + 1) * 128)
                pcs = [1] if gb == 0 else [0, 1]
                for h in range(2):
                    hs = slice(h * 64, (h + 1) * 64)
                    lp = ps_sc[h]
                    for q_i, pc in enumerate(pcs):
                        ksrc = KN[slp] if pc == 0 else KN[sl]
                        mm(lp[:, pc * 128:(pc + 1) * 128], ksrc[hs, :], qn[hs, ts_], [ksrc, qn], [lp], sig=(q_i == len(pcs) - 1))
                for h in range(2):
                    lp = ps_sc[h]
                    lo = 128 if gb == 0 else 0
                    act(Es[:, h * 256 + lo:(h + 1) * 256], lp[:, lo:256], AF.Exp, [lp], [Es])
                    tt("vector", Em[:, h * 256 + lo:(h + 1) * 256], Es[:, h * 256 + lo:(h + 1) * 256], EB[:, h * 256 + lo:(h + 1) * 256], ALU.mult, [Es, EB], [Em])
                if GS < 4:
                    continue
                for h in range(2):
                    Vx = VA if h == 0 else VB
                    ob = ps_in[h][:, n * 128:(n + 1) * 128]
                    for q_i, pc in enumerate(pcs):
                        vsrc = Vx[slp] if pc == 0 else Vx[sl]
                        mm(ob, vsrc[:, :], Em[:, (h * 2 + pc) * 128:(h * 2 + pc + 1) * 128], [vsrc, Em], [ps_in[h]], start=(q_i == 0), stop=(q_i == len(pcs) - 1), sig=(q_i == len(pcs) - 1))
            if GS < 5:
                continue
            rec = tmp[5]
            cp("vector", rec[0:64, :], ps_in[0][64:128, :], [ps_in[0]], [rec])
            cp("vector", rec[64:128, :], ps_in[1][0:64, :], [ps_in[1]], [rec])
            ts("vector", rec[:], rec[:], dp[:, 2:3], None, ALU.add, None, [rec, dp], [rec])
            recip("vector", rec[:], rec[:], [rec], [rec])
            yat = tmp[6]
            tt("vector", yat[0:64, :], ps_in[0][0:64, :], rec[0:64, :], ALU.mult, [ps_in[0], rec], [yat])
            tt("vector", yat[64:128, :], ps_in[1][64:128, :], rec[64:128, :], ALU.mult, [ps_in[1], rec], [yat])
            sga = tmp[7]
            act(sga[:], ga_[:, 1:TT_ + 1], AF.Silu, [ga_], [sga])
            tt("gpsimd", yT[1][:], yat[:], sga[:], ALU.mult, [yat, sga], [yT[1]])

            if STAGE < 7:
                continue
            tk0 = P.dma(lambda e, t0=t0: e.dma_start(out=a_in.ap()[0:128, t0:t0 + TT_], in_=yT[0][:]), reads=[yT[0]], writes=[bin_])
            tk1 = P.dma(lambda e, t0=t0: e.dma_start(out=a_in.ap()[128:256, t0:t0 + TT_], in_=yT[1][:]), reads=[yT[1]], writes=[bin_])
            if not fused:
                out_tickets.extend([tk0, tk1])
            if DBG:
                out_tickets.append(P.dma(lambda e, t0=t0: e.dma_start(out=dbg_y[0:128, t0:t0 + TT_], in_=yT[0][:]), reads=[yT[0]]))
                out_tickets.append(P.dma(lambda e, t0=t0: e.dma_start(out=dbg_y[128:256, t0:t0 + TT_], in_=yT[1][:]), reads=[yT[1]]))
        if STAGE >= 7 and fused:
            P.op("gpsimd", lambda e: e.collective_compute("AllGather", ALU.bypass, replica_groups=[list(range(NCORE))], ins=[a_in.ap().opt()], outs=[a_out.ap().opt()]), [bin_], [bout])
            for i in range(NT):
                t0 = i * TT_
                for r in range(NCORE):
                    P.dma(lambda e, t0=t0, r=r: e.dma_start(out=ya[:, 2 * r:2 * r + 2, :], in_=a_out.ap()[r * 256:(r + 1) * 256, t0:t0 + TT_].rearrange("(h p) t -> p h t", p=128)), reads=[bout], writes=[ya])
                for dc in range(2):
                    P.dma(lambda e, dc=dc, t0=t0: e.dma_start(out=xr[dc][:], in_=xres[dc * 128:(dc + 1) * 128, t0:t0 + TT_]), writes=[xr[dc]])
                    pacc = ps_in[dc]
                    for j in range(KC):
                        mm(pacc[:, :], Wo[:, j, dc * 128:(dc + 1) * 128], ya[:, j, :], [Wo, ya], [pacc], start=(j == 0), stop=(j == KC - 1), sig=(j == KC - 1))
                    tt("vector", ost[dc][:], pacc[:, :], xr[dc][:], ALU.add, [pacc, xr[dc]], [ost[dc]])
                    out_tickets.append(P.dma(lambda e, dc=dc, t0=t0: e.dma_start(out=outT[dc * 128:(dc + 1) * 128, t0:t0 + TT_], in_=ost[dc][:]), reads=[ost[dc]]))
        P.wait_all("sync", out_tickets)
        P.emit()
    return nc


def build_out(S):
    NT = S // TT_
    nc = bass.Bass("TRN2", target_bir_lowering=False)
    din = lambda n, sh, dt=F32: nc.dram_tensor(n, sh, dt, kind="ExternalInput").ap()
    yall = din("yall", [D, S], BF16)
    xres = din("xres", [256, S])
    woc = din("woc", [D, 256])
    outT = nc.dram_tensor("outT", [256, S], F32, kind="ExternalOutput").ap()
    es = ExitStack()
    with es:
        P = Prog(nc, es)
        cnt = [0]

        def sb(shape, dt=F32):
            cnt[0] += 1
            return T(es.enter_context(nc.sbuf_tensor("sb%d" % cnt[0], shape, dt)))

        def psb(shape, dt=F32):
            cnt[0] += 1
            return T(es.enter_context(nc.psum_tensor("ps%d" % cnt[0], shape, dt)))

        Wo = sb([128, KC, 256], BF16)
        wst = [sb([128, 256]) for _ in range(2)]
        for kc in range(KC):
            s_ = wst[kc % 2]
            P.dma(lambda e, kc=kc, s_=s_: e.dma_start(out=s_[:], in_=woc[kc * 128:(kc + 1) * 128, :]), writes=[s_])
            P.op("vector", lambda e, kc=kc, s_=s_: e.tensor_copy(out=Wo[:, kc, :], in_=s_[:]), [s_], [Wo])
        ya = [sb([128, KC, TT_], BF16) for _ in range(2)]
        xr = [[sb([128, TT_]) for _ in range(2)] for _ in range(2)]
        ost = [[sb([128, TT_]) for _ in range(2)] for _ in range(2)]
        pacc = [psb([128, 512]) for _ in range(4)]
        tickets = []
        for i in range(NT):
            t0 = i * TT_
            yb = ya[i % 2]
            P.dma(lambda e, t0=t0, yb=yb: e.dma_start(out=yb[:], in_=yall[:, t0:t0 + TT_].rearrange("(j p) t -> p j t", p=128)), writes=[yb])
            for dc in range(2):
                xb_ = xr[i % 2][dc]
                ob_ = ost[i % 2][dc]
                pc_ = pacc[(i % 2) * 2 + dc]
                P.dma(lambda e, t0=t0, dc=dc, xb_=xb_: e.dma_start(out=xb_[:], in_=xres[dc * 128:(dc + 1) * 128, t0:t0 + TT_]), writes=[xb_])
                for j in range(KC):
                    P.op("tensor", lambda e, j=j, dc=dc, yb=yb, pc_=pc_: e.matmul(pc_[:, :], lhsT=Wo[:, j, dc * 128:(dc + 1) * 128], rhs=yb[:, j, :], start=(j == 0), stop=(j == KC - 1)), [Wo, yb], [pc_], sig=(j == KC - 1))
                P.op("vector", lambda e, pc_=pc_, xb_=xb_, ob_=ob_: e.tensor_tensor(out=ob_[:], in0=pc_[:, :], in1=xb_[:], op=ALU.add), [pc_, xb_], [ob_])
                tickets.append(P.dma(lambda e, t0=t0, dc=dc, ob_=ob_: e.dma_start(out=outT[dc * 128:(dc + 1) * 128, t0:t0 + TT_], in_=ob_[:]), reads=[ob_], eng="scalar"))
        P.wait_all("sync", tickets)
        P.emit()
    return nc


D_RWKV = 1024
RWKV_COLS = 4 * 1024 + 128
D_ATT = 1024


def _t5_bucket(dist):
    n = np.maximum(dist, 0)
    nf = np.maximum(n, 1).astype(np.float32)
    large = 16 + (np.log(nf / 16) / math.log(128 / 16) * 16).astype(np.int32)
    large = np.minimum(large, 31)
    return np.where(n < 16, n, large)


def _consts():
    c = np.zeros((128, 1728), np.float32)
    c[:, 0:128] = np.eye(128, dtype=np.float32)
    c[0:64, 128:192] = 1.0
    c[64:128, 192:256] = 1.0
    row = (np.arange(128) % 64)[:, None]
    col = np.arange(64)[None, :]
    lower = (col < row).astype(np.float32)
    upper = (col > row).astype(np.float32)
    upper_i = (col >= row).astype(np.float32)
    for cc in range(2):
        c[:, 256 + cc * 192 + 0:256 + cc * 192 + 64] = lower
        c[:, 256 + cc * 192 + 64:256 + cc * 192 + 128] = upper
        c[:, 256 + cc * 192 + 128:256 + cc * 192 + 192] = upper_i
        c[:, 640 + cc * 128 + 0:640 + cc * 128 + 64] = upper
        c[:, 640 + cc * 128 + 64:640 + cc * 128 + 128] = upper_i
    c[:, 896:960] = (col == row).astype(np.float32)
    rm = np.ones(512, np.float32)
    rm[0::64] = 0.0
    c[:, 960:1472] = rm[None, :]
    kj = np.arange(128)[:, None]
    qi = np.arange(128)[None, :]
    c[:, 1472:1600] = (qi < kj).astype(np.float32)
    c[:, 1600:1728] = (kj <= qi).astype(np.float32)
    return c


def _bias_T(rel_bias, heads):
    out = np.zeros((128, 512), np.float32)
    kj = np.arange(128)[:, None]
    qi = np.arange(128)[None, :]
    for hi, hd in enumerate(heads):
        for pc in range(2):
            dist = (128 + qi - kj) if pc == 0 else (qi - kj)
            b = _t5_bucket(dist)
            out[:, (hi * 2 + pc) * 128:(hi * 2 + pc + 1) * 128] = rel_bias[b, hd]
    return out


def _prep_inputs(S, x, norm_w, w_in, w_out, mu_rwkv, w0, w2, a0, a2, k_k, k_a, r_k,
                 lnx_w, lnx_b, q_norm_w, k_norm_w, sinks, rel_bias):
    f = lambda a: np.asarray(a, dtype=np.float32)
    x, norm_w, w_in, w_out = f(x)[0], f(norm_w)[0], f(w_in)[0], f(w_out)[0]
    mu, w0, w2, a0, a2 = f(mu_rwkv)[0], f(w0)[0], f(w2)[0], f(a0)[0], f(a2)[0]
    k_k, k_a, r_k = f(k_k)[0], f(k_a)[0], f(r_k)[0].reshape(-1)
    lnx_w, lnx_b, qw, kw, sinks, rel_bias = f(lnx_w)[0], f(lnx_b)[0], f(q_norm_w)[0], f(k_norm_w)[0], f(sinks)[0], f(rel_bias)
    xT = np.ascontiguousarray(x.T)
    cst = _consts()
    maps = []
    for c in range(NCORE):
        hsl = slice(c * 128, (c + 1) * 128)
        kvh = c // 4
        A0 = RWKV_COLS
        cols = np.concatenate([
            np.arange(c * 128, (c + 1) * 128),
            D_RWKV + np.arange(c * 128, (c + 1) * 128),
            2 * D_RWKV + np.arange(c * 128, (c + 1) * 128),
            3 * D_RWKV + np.arange(c * 128, (c + 1) * 128),
            4 * D_RWKV + np.arange(0, 128),
            A0 + np.arange(c * 128, (c + 1) * 128),
            A0 + D_ATT + kvh * 64 + np.arange(64),
            A0 + D_ATT + 128 + kvh * 64 + np.arange(64),
            A0 + D_ATT + 256 + np.arange(c * 128, (c + 1) * 128),
        ])
        wc = np.ascontiguousarray(w_in[:, cols])
        rows = np.concatenate([np.concatenate([r * 128 + np.arange(128), 1024 + r * 128 + np.arange(128)]) for r in range(NCORE)])
        woc = np.ascontiguousarray(w_out[rows][:, c * 256:(c + 1) * 256])
        pp = np.zeros((128, 32), np.float32)
        for j in range(5):
            pp[:, j] = mu[cols[j * 128:(j + 1) * 128]]
        pp[:, 5] = w0[hsl]
        pp[:, 6] = a0[hsl]
        pp[:, 7] = k_k[hsl]
        pp[:, 8] = k_a[hsl]
        pp[:, 9] = r_k[hsl]
        pp[:, 10] = lnx_w[hsl]
        pp[:, 11] = lnx_b[hsl]
        pp[:, 12] = np.concatenate([qw, qw])
        pp[:, 13] = np.concatenate([kw, kw])
        pp[0:64, 14] = sinks[2 * c]
        pp[64:128, 14] = sinks[2 * c + 1]
        pp[0:64, 15] = sinks[2 * c + 1]
        pp[64:128, 15] = sinks[2 * c]
        pp[:, 16:32] = norm_w.reshape(16, 128).T
        w2a2 = np.concatenate([w2[:, hsl], a2[:, hsl]], axis=0)
        maps.append({
            "xT": xT,
            "xres": np.ascontiguousarray(xT[c * 256:(c + 1) * 256, :]),
            "wc": wc, "woc": woc, "w2a2": np.ascontiguousarray(w2a2), "pp": pp, "cst": cst,
            "abias": _bias_T(rel_bias, [2 * c, 2 * c + 1]),
        })
    return maps


FUSED = bool(int(os.environ.get("KFUSED", "0")))


def kernel(**inputs):
    S = inputs["x"].shape[1]
    maps = _prep_inputs(S, **inputs)
    if FUSED:
        nc = build(S, fused=True)
        res = run_bass_kernel_spmd(nc, maps, core_ids=list(range(NCORE)))
        if int(os.environ.get('KDBG', '0')):
            global DBG_Y
            DBG_Y = [np.asarray(res.results[c]["dbg_y"]).astype(np.float32) for c in range(NCORE)]
    else:
        m1 = [{k: v for k, v in m.items() if k not in ("xres", "woc")} for m in maps]
        nc = build(S, fused=False)
        res = run_bass_kernel_spmd(nc, m1, core_ids=list(range(NCORE)))
        yall = np.concatenate([np.asarray(res.results[c]["yc"]) for c in range(NCORE)], axis=0)
        m2 = [{"yall": yall, "xres": m["xres"], "woc": m["woc"]} for m in maps]
        nc2 = build_out(S)
        res = run_bass_kernel_spmd(nc2, m2, core_ids=list(range(NCORE)))
    outT = np.concatenate([res.results[c]["outT"] for c in range(NCORE)], axis=0)
    return np.ascontiguousarray(outT.T)[None].astype(np.float32)
```

```python
import math
import os
import numpy as np
import concourse.bass as bass
import concourse.mybir as mybir
from concourse.bass_utils import run_bass_kernel_spmd
from contextlib import ExitStack

F32 = mybir.dt.float32
BF16 = mybir.dt.bfloat16
ALU = mybir.AluOpType
AF = mybir.ActivationFunctionType

NCORE = 8
D = 2048
KC = D // 128
TT_ = 512
SAME_ENGINE_SYNC = bool(int(os.environ.get("KSES", "1")))
NDS = 24
ANNOTATE = bool(int(os.environ.get('KANN', '0')))
C0 = math.exp(-0.5)
NORM_EPS = 1e-6
import os
STAGE = int(os.environ.get('KSTAGE', '99'))
SUB = int(os.environ.get('KSUB', '99'))
GS = int(os.environ.get('KGS', '99'))
KM = int(os.environ.get('KM', '31'))
KNS = int(os.environ.get('KNS', '6'))
KNE = int(os.environ.get('KNE', '1'))
LNX_EPS = 64e-5


class Buf:
    __slots__ = ("w", "r")

    def __init__(self):
        self.w = None
        self.r = []


class Prog:
    ENG = ["tensor", "vector", "scalar", "gpsimd", "sync"]

    def __init__(self, nc, es):
        self.nc = nc
        self.ops = {e: [] for e in self.ENG}
        self.cnt = {e: 0 for e in self.ENG}
        self.sem = {}
        for e in self.ENG:
            if e != "sync":
                self.sem[e] = es.enter_context(nc.semaphore("s_" + e))
        for i in range(NDS):
            self.sem["d%d" % i] = es.enter_context(nc.semaphore("s_d%d" % i))
        self.seen = {e: {} for e in self.ENG}
        self.ndma = 0
        self.tag = "setup"

    def _deps(self, eng, reads, writes):
        waits = {}

        def need(t):
            if t is None:
                return
            e, v = t
            if e == eng:
                if eng == "tensor" or not SAME_ENGINE_SYNC or v > self.cnt[eng]:
                    return
            if waits.get(e, 0) < v:
                waits[e] = v

        for b in reads:
            need(b.w)
        for b in writes:
            need(b.w)
            for t in b.r:
                need(t)
        w2 = []
        for e, v in waits.items():
            if self.seen[eng].get(e, 0) < v:
                self.seen[eng][e] = v
                w2.append((e, v))
        return w2

    def op(self, eng, fn, reads=(), writes=(), sig=True):
        reads = [getattr(b, "b", b) for b in reads]
        writes = [getattr(b, "b", b) for b in writes]
        w2 = self._deps(eng, reads, writes)
        ticket = (eng, self.cnt[eng] + 1)
        if sig:
            self.cnt[eng] += 1
        for b in reads:
            b.r.append(ticket)
        for b in writes:
            b.w = ticket
            b.r = []
        self.ops[eng].append((fn, w2, (eng, 1) if sig else None, self.tag))

    def dma(self, fn, reads=(), writes=(), eng="sync"):
        reads = [getattr(b, "b", b) for b in reads]
        writes = [getattr(b, "b", b) for b in writes]
        w2 = self._deps(eng, reads, writes)
        k = self.ndma % NDS
        v = 16 * (self.ndma // NDS + 1)
        self.ndma += 1
        ticket = ("d%d" % k, v)
        for b in reads:
            b.r.append(ticket)
        for b in writes:
            b.w = ticket
            b.r = []
        self.ops[eng].append((fn, w2, ("d%d" % k, 16), self.tag))
        return ticket

    def wait_all(self, eng, tickets):
        w2 = []
        for e, v in tickets:
            if self.seen[eng].get(e, 0) < v:
                self.seen[eng][e] = v
                w2.append((e, v))
        self.ops[eng].append((None, w2, None, self.tag))

    def emit(self):
        nc = self.nc
        with nc.Block() as block:
            def mk(ename):
                def body(eng):
                    for fn, waits, sig, tag in self.ops[ename]:
                        for e, v in waits:
                            w = eng.wait_ge(self.sem[e], v)
                            if ANNOTATE:
                                w.annotate("W:" + tag + ":" + e)
                        if fn is None:
                            continue
                        ins = fn(eng)
                        if ANNOTATE:
                            ins.annotate(tag)
                        if sig is not None:
                            ins.then_inc(self.sem[sig[0]], sig[1])
                return body
            block.tensor(mk("tensor"))
            block.vector(mk("vector"))
            block.scalar(mk("scalar"))
            block.gpsimd(mk("gpsimd"))
            block.sync(mk("sync"))


class T:
    def __init__(self, t):
        self.t = t
        self.b = Buf()

    def __getitem__(self, k):
        return self.t[k]

    def sub(self):
        v = T(self.t)
        return v


def build(S, fused=True):
    NT = S // TT_
    nc = bass.Bass("TRN2", target_bir_lowering=False)
    din = lambda n, sh, dt=F32: nc.dram_tensor(n, sh, dt, kind="ExternalInput").ap()
    xT = din("xT", [D, S])
    wc = din("wc", [D, 1024])
    if fused:
        xres = din("xres", [256, S])
        woc = din("woc", [D, 256])
    w2a2 = din("w2a2", [128, 128])
    pp = din("pp", [128, 32])
    cst = din("cst", [128, 1728])
    abias = din("abias", [128, 512])
    if fused:
        outT = nc.dram_tensor("outT", [256, S], F32, kind="ExternalOutput").ap()
        a_in = nc.dram_tensor("a_in", [256, S], BF16)
    else:
        a_in = nc.dram_tensor("yc", [256, S], BF16, kind="ExternalOutput")
    DBG = bool(int(os.environ.get('KDBG', '0')))
    if DBG:
        dbg_y = nc.dram_tensor("dbg_y", [256, S], BF16, kind="ExternalOutput").ap()
    if fused:
        a_out = nc.dram_tensor("a_out", [NCORE * 256, S], BF16)

    es = ExitStack()
    with es:
        P = Prog(nc, es)
        cnt = [0]

        def sb(shape, dt=F32):
            cnt[0] += 1
            return T(es.enter_context(nc.sbuf_tensor("sb%d" % cnt[0], shape, dt)))

        def psb(shape, dt=F32):
            cnt[0] += 1
            return T(es.enter_context(nc.psum_tensor("ps%d" % cnt[0], shape, dt)))

        def tt(eng, out, in0, in1, op, R, W):
            P.op(eng, lambda e: e.tensor_tensor(out=out, in0=in0, in1=in1, op=op), R, W)

        def stt(eng, out, in0, scalar, in1, op0, op1, R, W):
            P.op(eng, lambda e: e.scalar_tensor_tensor(out=out, in0=in0, scalar=scalar, in1=in1, op0=op0, op1=op1), R, W)

        def ts(eng, out, in0, s1, s2, op0, op1, R, W):
            if s2 is None:
                P.op(eng, lambda e: e.tensor_scalar(out=out, in0=in0, scalar1=s1, scalar2=None, op0=op0), R, W)
            else:
                P.op(eng, lambda e: e.tensor_scalar(out=out, in0=in0, scalar1=s1, scalar2=s2, op0=op0, op1=op1), R, W)

        def cp(eng, out, in_, R, W):
            if eng == "scalar":
                P.op(eng, lambda e: e.copy(out=out, in_=in_), R, W)
            else:
                P.op(eng, lambda e: e.tensor_copy(out=out, in_=in_), R, W)

        def act(out, in_, func, R, W, bias=None, scale=None):
            kw = {}
            if bias is not None:
                kw["bias"] = bias
            if scale is not None:
                kw["scale"] = scale
            P.op("scalar", lambda e: e.activation(out=out, in_=in_, func=func, **kw), R, W)

        def mm(out, lhsT, rhs, R, W, start=True, stop=True, sig=True):
            P.op("tensor", lambda e: e.matmul(out, lhsT=lhsT, rhs=rhs, start=start, stop=stop), R, W, sig=sig)

        def recip(eng, out, in_, R, W):
            P.op(eng, lambda e: e.reciprocal(out=out, in_=in_), R, W)

        cst_t = sb([128, 1728])
        pp_t = sb([128, 32])
        dp = sb([128, 8])
        w2a2_t = sb([128, 128])
        ab_t = sb([128, 512])
        EB = sb([128, 512], BF16)
        identb = sb([128, 128], BF16)
        onesbd_b = sb([128, 128], BF16)
        ones_b = sb([128, 128], BF16)
        Wb = sb([128, KC, 1024], BF16)
        Wo = sb([128, KC, 256], BF16) if fused else None
        P.dma(lambda e: e.dma_start(out=cst_t[:], in_=cst[:, :]), writes=[cst_t])
        P.dma(lambda e: e.dma_start(out=pp_t[:], in_=pp[:, :]), writes=[pp_t])
        P.dma(lambda e: e.dma_start(out=w2a2_t[:], in_=w2a2[:, :]), writes=[w2a2_t])
        P.dma(lambda e: e.dma_start(out=ab_t[:], in_=abias[:, :]), writes=[ab_t])
        ident_f = cst_t[:, 0:128]
        onesbd_f = cst_t[:, 128:256]
        maskA = cst_t[:, 256:640]
        maskB = cst_t[:, 640:896]
        i2_f = cst_t[:, 896:960]
        resetm = cst_t[:, 960:1472]
        cp("vector", identb[:], ident_f, [cst_t], [identb])
        cp("vector", onesbd_b[:], onesbd_f, [cst_t], [onesbd_b])
        P.op("vector", lambda e: e.memset(ones_b[:], 1.0), [], [ones_b])
        ts("vector", dp[:, 0:1], pp_t[:, 8:9], -1.0, 1.0, ALU.mult, ALU.add, [pp_t], [dp])
        ts("vector", dp[:, 1:2], pp_t[:, 12:13], 0.125, None, ALU.mult, None, [pp_t], [dp])
        act(dp[:, 2:3], pp_t[:, 14:15], AF.Exp, [pp_t], [dp])
        act(dp[:, 5:6], pp_t[:, 15:16], AF.Exp, [pp_t], [dp])
        P.op("vector", lambda e: e.memset(dp[:, 3:4], NORM_EPS), [], [dp])
        P.op("vector", lambda e: e.memset(dp[:, 4:5], LNX_EPS), [], [dp])
        xst = [sb([128, TT_]) for _ in range(4)]
        tmp = [sb([128, TT_]) for _ in range(12)]
        ebf = tmp[0]
        act(ebf[:], ab_t[:], AF.Exp, [ab_t], [ebf])
        for h in range(2):
            tt("vector", EB[:, h * 256:(h + 1) * 256], ebf[:, h * 256:(h + 1) * 256], cst_t[:, 1472:1728], ALU.mult, [ebf, cst_t], [EB])
        k_ = 0
        for kc in range(KC):
            for hf in range(2):
                s_ = xst[k_ % 4]
                k_ += 1
                P.dma(lambda e, kc=kc, hf=hf, s_=s_: e.dma_start(out=s_[:], in_=wc[kc * 128:(kc + 1) * 128, hf * 512:(hf + 1) * 512]), writes=[s_])
                if k_ % 2 == 0:
                    ts("vector", Wb[:, kc, hf * 512:(hf + 1) * 512], s_[:], pp_t[:, 16 + kc:17 + kc], None, ALU.mult, None, [s_, pp_t], [Wb])
                else:
                    act(Wb[:, kc, hf * 512:(hf + 1) * 512], s_[:], AF.Copy, [s_, pp_t], [Wb], scale=pp_t[:, 16 + kc:17 + kc])
        for kc in range(KC if fused else 0):
            s_ = xst[k_ % 4]
            k_ += 1
            P.dma(lambda e, kc=kc, s_=s_: e.dma_start(out=s_[:, 0:256], in_=woc[kc * 128:(kc + 1) * 128, :]), writes=[s_])
            cp("vector" if kc % 2 == 0 else "gpsimd", Wo[:, kc, :], s_[:, 0:256], [s_], [Wo])

        zs = [sb([128, TT_ + 1]) for _ in range(5)]
        for j in range(5):
            P.op("gpsimd", lambda e, j=j: e.memset(zs[j][:, 0:1], 0.0), [], [zs[j]])
        zsA = [[sb([128, TT_]) for _ in range(3)] for _ in range(2)]
        Sm = sb([128, 64])
        Sst = sb([128, 64], BF16)
        P.op("vector", lambda e: e.memset(Sm[:], 0.0), [], [Sm])
        P.op("vector", lambda e: e.memset(Sst[:], 0.0), [], [Sst])
        NSL = 8
        KN = [sb([128, 128], BF16) for _ in range(NSL)]
        VA = [sb([128, 128], BF16) for _ in range(NSL)]
        VB = [sb([128, 128], BF16) for _ in range(NSL)]
        for s_ in range(NSL):
            P.op("gpsimd", lambda e, s_=s_: e.memset(VA[s_][:, 64:128], 1.0), [], [VA[s_]])
            P.op("gpsimd", lambda e, s_=s_: e.memset(VB[s_][:, 0:64], 1.0), [], [VB[s_]])

        xsq = [sb([128, TT_], BF16) for _ in range(2)]
        xb = sb([128, KC, TT_], BF16)
        rstd = sb([128, TT_])
        zsh = [sb([128, TT_]) for _ in range(5)]
        tb = [sb([128, TT_], BF16) for _ in range(4)]
        AR = [sb([128, 8, 2, 64], BF16) for _ in range(2)]
        YT = [sb([128, 8, 2, 64], BF16) for _ in range(2)]
        YH = [sb([128, 8, 2, 64], BF16) for _ in range(2)]
        gC = [sb([128, 8]) for _ in range(2)]
        vb = [sb([128, TT_], BF16) for _ in range(2)]
        bonb = [sb([128, TT_]) for _ in range(2)]
        silb = [sb([128, TT_]) for _ in range(2)]
        SA = [sb([128, 2, 192], BF16) for _ in range(2)]
        SBs = [sb([128, 2, 128], BF16) for _ in range(2)]
        NLT = [[sb([128, 2, 192], BF16) for _ in range(2)] for _ in range(2)]
        TRs = [sb([128, 2, 192], BF16) for _ in range(2)]
        Ybf = sb([128, 64], BF16)
        Ubf = sb([128, 64], BF16)
        Ot = sb([128, TT_])
        fx = [sb([128, TT_]) for _ in range(2)]
        gx = [sb([128, TT_]) for _ in range(4)]
        pvs = sb([128, 4, 256])
        qn = sb([128, TT_], BF16)
        Es = sb([128, 512], BF16)
        Em = sb([128, 512], BF16)
        yT = [sb([128, TT_], BF16) for _ in range(2)]
        if fused:
            ya = xb
            xr = [tmp[0], tmp[1]]
            ost = [tmp[2], tmp[3]]

        ps_in = [psb([128, 512]) for _ in range(2)]
        ps_sc = [psb([128, 512]) for _ in range(2)]
        ps_ch = psb([128, 512])
        ps_chN = ps_ch.sub()
        ps_chC = ps_ch.sub()
        ps_o = psb([128, 512])
        gA = psb([128, 512])
        gB = psb([128, 512])

        out_tickets = []
        bin_ = Buf()
        bout = Buf()
        prog = {"fe": 0, "prep": (0, 0), "chain_g": 0}
        v3 = lambda ap_: ap_.rearrange("p (c t) -> p c t", t=64)

        def fe(i):
            p = i % 2
            t0 = i * TT_
            stat = ps_in[1]
            for kc in range(KC):
                s_ = xst[kc % 4]
                q_ = xsq[kc % 2]
                P.dma(lambda e, kc=kc, s_=s_, t0=t0: e.dma_start(out=s_[:], in_=xT[kc * 128:(kc + 1) * 128, t0:t0 + TT_]), writes=[s_])
                act(q_[:], s_[:], AF.Square, [s_], [q_])
                cp("vector" if kc % 2 == 0 else "scalar", xb[:, kc, :], s_[:], [s_], [xb])
                mm(stat[:, :], ones_b[:], q_[:], [ones_b, q_], [stat], start=(kc == 0), stop=(kc == KC - 1), sig=True)
                if kc % 4 == 3:
                    yield
            act(rstd[:], stat[:, :], AF.Sqrt, [stat, dp], [rstd], bias=dp[:, 3:4], scale=1.0 / D)
            recip("vector", rstd[:], rstd[:], [rstd], [rstd])
            yield
            for j in range(8):
                pacc = ps_in[j % 2]
                for kc in range(KC):
                    mm(pacc[:, :], Wb[:, kc, j * 128:(j + 1) * 128], xb[:, kc, :], [Wb, xb], [pacc], start=(kc == 0), stop=(kc == KC - 1), sig=(kc % 4 == 3))
                    if kc % 4 == 3 and kc != KC - 1:
                        yield
                if j < 5:
                    tt("vector", zs[j][:, 1:TT_ + 1], pacc[:, :], rstd[:], ALU.mult, [pacc, rstd], [zs[j]])
                else:
                    tt("vector", zsA[p][j - 5][:], pacc[:, :], rstd[:], ALU.mult, [pacc, rstd], [zsA[p][j - 5]])
                yield
            for j in range(5):
                d_ = tmp[0]
                tt("gpsimd", d_[:], zs[j][:, 0:TT_], zs[j][:, 1:TT_ + 1], ALU.subtract, [zs[j]], [d_])
                yield
                stt("vector", zsh[j][:], d_[:], pp_t[:, j:j + 1], zs[j][:, 1:TT_ + 1], ALU.mult, ALU.add, [d_, zs[j], pp_t], [zsh[j]])
                yield
                cp("scalar", zs[j][:, 0:1], zs[j][:, TT_:TT_ + 1], [zs[j]], [zs[j]])
                yield
            yield
            r_, k_, v_, g_, lo_ = zsh
            AR_, YT_, YH_, gC_, vb_, bon, sil = AR[p], YT[p], YH[p], gC[p], vb[p], bonb[p], silb[p]
            tl = tmp[0]
            act(tl[0:64, :], lo_[0:64, :], AF.Tanh, [lo_], [tl])
            yield
            mm(ps_in[0][:, :], w2a2_t[0:64, :], tl[0:64, :], [w2a2_t, tl], [ps_in[0]])
            yield
            mm(ps_in[1][:, :], w2a2_t[64:128, :], lo_[64:128, :], [w2a2_t, lo_], [ps_in[1]])
            yield
            sg = tmp[1]
            a_ = tmp[2]
            act(sg[:], ps_in[0][:, :], AF.Sigmoid, [ps_in[0], pp_t], [sg], bias=pp_t[:, 5:6])
            yield
            act(a_[:], ps_in[1][:, :], AF.Sigmoid, [ps_in[1], pp_t], [a_], bias=pp_t[:, 6:7])
            yield
            yield
            cum = tmp[3]
            P.op("vector", lambda e: e.tensor_tensor_scan(out=cum[:], data0=resetm, data1=sg[:], initial=0.0, op0=ALU.mult, op1=ALU.add), [sg, cst_t], [cum])
            yield
            cumx = tmp[4]
            tt("gpsimd", cumx[:], cum[:], sg[:], ALU.subtract, [cum, sg], [cumx])
            yield
            E1, E2, E3, Eh = tmp[5], tmp[6], tmp[7], tmp[8]
            act(E1[:], cum[:], AF.Exp, [cum], [E1], scale=-C0)
            yield
            act(E2[:], cum[:], AF.Exp, [cum], [E2], scale=C0)
            yield
            act(E3[:], cumx[:], AF.Exp, [cumx], [E3], scale=-C0)
            yield
            cum3 = v3(cum[:])
            d2 = cumx
            tt("vector", v3(d2[:]), cum3, cum3[:, :, 63:64].to_broadcast([128, 8, 64]), ALU.subtract, [cum, cumx], [cumx])
            yield
            act(Eh[:], d2[:], AF.Exp, [cumx], [Eh], scale=C0)
            yield
            act(gC_[:], cum3[:, :, 63], AF.Exp, [cum], [gC_], scale=-C0)
            yield
            yield
            act(tb[0][:], k_[:], AF.Square, [k_, pp_t], [tb[0]], scale=pp_t[:, 7:8])
            yield
            mm(ps_in[0][:, :], onesbd_b[:], tb[0][:], [onesbd_b, tb[0]], [ps_in[0]])
            yield
            rs = tmp[9]
            act(rs[:], ps_in[0][:, :], AF.Sqrt, [ps_in[0]], [rs])
            yield
            ts("vector", rs[:], rs[:], 1e-12, None, ALU.max, None, [rs], [rs])
            yield
            recip("vector", rs[:], rs[:], [rs], [rs])
            yield
            kk = tmp[10]
            stt("vector", kk[:], k_[:], pp_t[:, 7:8], rs[:], ALU.mult, ALU.mult, [k_, pp_t, rs], [kk])
            yield
            ka = tmp[11]
            tt("gpsimd", ka[:], kk[:], a_[:], ALU.mult, [kk, a_], [ka])
            yield
            yield
            stt("vector", AR_[:, :, 0, :], v3(kk[:]), -1.0, v3(E3[:]), ALU.mult, ALU.mult, [kk, E3], [AR_])
            yield
            tt("gpsimd", AR_[:, :, 1, :], v3(r_[:]), v3(E1[:]), ALU.mult, [r_, E1], [AR_])
            yield
            tt("vector", YT_[:, :, 0, :], v3(ka[:]), v3(E2[:]), ALU.mult, [ka, E2], [YT_])
            yield
            tt("gpsimd", YH_[:, :, 0, :], v3(ka[:]), v3(Eh[:]), ALU.mult, [ka, Eh], [YH_])
            yield
            kp = tmp[9]
            ts("vector", kp[:], a_[:], pp_t[:, 8:9], dp[:, 0:1], ALU.mult, ALU.add, [a_, pp_t, dp], [kp])
            yield
            tt("vector", kp[:], kp[:], k_[:], ALU.mult, [kp, k_], [kp])
            yield
            yield
            tt("vector", YT_[:, :, 1, :], v3(kp[:]), v3(E2[:]), ALU.mult, [kp, E2], [YT_])
            yield
            tt("gpsimd", YH_[:, :, 1, :], v3(kp[:]), v3(Eh[:]), ALU.mult, [kp, Eh], [YH_])
            yield
            stt("vector", tb[1][:], r_[:], pp_t[:, 9:10], kp[:], ALU.mult, ALU.mult, [r_, pp_t, kp], [tb[1]])
            yield
            mm(ps_in[1][:, :], onesbd_b[:], tb[1][:], [onesbd_b, tb[1]], [ps_in[1]])
            yield
            tt("vector", bon[:], ps_in[1][:, :], v_[:], ALU.mult, [ps_in[1], v_], [bon])
            yield
            cp("scalar", vb_[:], v_[:], [v_], [vb_])
            yield
            act(sil[:], g_[:], AF.Silu, [g_], [sil])
            yield
            prog["fe"] = i + 1
            yield

        def prep(i):
            p = i % 2
            AR_, YT_, YH_, vb_ = AR[p], YT[p], YH[p], vb[p]
            pa, pb = ps_sc
            for m in range(4):
                while prog["chain_g"] < 4 * i + m - 1:
                    yield "blocked"
                q = m % 2
                cs = [2 * m, 2 * m + 1]
                SA_, SB_, TR_ = SA[q], SBs[q], TRs[q]
                for cc, c in enumerate(cs):
                    for h in range(2):
                        hs = slice(h * 64, (h + 1) * 64)
                        last = (cc == 1 and h == 1)
                        arc = AR_[hs, c, :, :].rearrange("p a t -> p (a t)")
                        mm(pa[hs, cc * 192:cc * 192 + 64], AR_[hs, c, 0, :], YT_[hs, c, 0, :], [AR_, YT_], [pa], sig=False)
                        mm(pa[hs, cc * 192 + 64:cc * 192 + 192], YT_[hs, c, 0, :], arc, [AR_, YT_], [pa], sig=False)
                        mm(pb[hs, cc * 128:cc * 128 + 128], YT_[hs, c, 1, :], arc, [AR_, YT_], [pb], sig=last)
                tt("vector", SA_[:].rearrange("p c x -> p (c x)"), pa[:, 0:384], maskA, ALU.mult, [pa, cst_t], [SA_])
                tt("vector", SB_[:].rearrange("p c x -> p (c x)"), pb[:, 0:256], maskB, ALU.mult, [pb, cst_t], [SB_])
                yield
                for cc, c in enumerate(cs):
                    for h in range(2):
                        hs = slice(h * 64, (h + 1) * 64)
                        last = (cc == 1 and h == 1)
                        idh = identb[hs, h * 64:(h + 1) * 64]
                        mm(pb[hs, 256 + cc * 64:256 + cc * 64 + 64], vb_[hs, c * 64:(c + 1) * 64], idh, [vb_, identb], [pb], sig=False)
                        mm(pa[hs, 384 + cc * 64:384 + cc * 64 + 64], YH_[hs, c, 0, :], idh, [YH_, identb], [pa], sig=False)
                        mm(pb[hs, 384 + cc * 64:384 + cc * 64 + 64], YH_[hs, c, 1, :], idh, [YH_, identb], [pb], sig=last)
                for cc in range(2):
                    cp("scalar", TR_[:, cc, 0:64], pb[:, 256 + cc * 64:256 + cc * 64 + 64], [pb], [TR_])
                    cp("scalar", TR_[:, cc, 64:128], pa[:, 384 + cc * 64:384 + cc * 64 + 64], [pa], [TR_])
                    cp("scalar", TR_[:, cc, 128:192], pb[:, 384 + cc * 64:384 + cc * 64 + 64], [pb], [TR_])
                yield
                NL = NLT[q]
                cur = 0
                cp("vector", NL[0][:, :, 0:128], SA_[:, :, 0:128], [SA_], [NL[0]])
                tt("vector", NL[0][:, :, 128:192], SA_[:, :, 64:128], i2_f.unsqueeze(1).to_broadcast([128, 2, 64]), ALU.add, [SA_, cst_t], [NL[0]])
                for stg in range(1, 7):
                    pn = ps_ch
                    src = NL[cur]
                    need_sq = stg <= 5
                    need_lt = stg <= 4
                    need_t = stg >= 2
                    for cc in range(2):
                        for h in range(2):
                            hs = slice(h * 64, (h + 1) * 64)
                            last = (cc == 1 and h == 1)
                            Lc = src[hs, cc, 0:64]
                            LTc = src[hs, cc, 64:128]
                            Tc = src[hs, cc, 128:192]
                            idh = identb[hs, h * 64:(h + 1) * 64]
                            if need_sq:
                                mm(pn[hs, cc * 192:cc * 192 + 64], LTc, Lc, [src], [pn], sig=False)
                            if need_lt:
                                mm(pn[hs, cc * 192 + 64:cc * 192 + 128], Lc, LTc, [src], [pn], sig=False)
                            mm(pn[hs, cc * 192 + 128:cc * 192 + 192], idh, Tc, [src, identb], [pn], start=True, stop=(not need_t), sig=(last and not need_t))
                            if need_t:
                                mm(pn[hs, cc * 192 + 128:cc * 192 + 192], Lc, Tc, [src], [pn], start=False, stop=True, sig=last)
                    nxt = 1 - cur
                    lo = 0 if need_sq else 128
                    pv = ps_ch[:, 0:384].rearrange("p (c x) -> p c x", x=192)[:, :, lo:192]
                    dst = NL[nxt]
                    if stg % 2 == 0:
                        P.op("scalar", lambda e, dst=dst, lo=lo, pv=pv: e.copy(out=dst[:, :, lo:192], in_=pv), [pn], [dst])
                    else:
                        P.op("vector", lambda e, dst=dst, lo=lo, pv=pv: e.tensor_copy(out=dst[:, :, lo:192], in_=pv), [pn], [dst])
                    cur = nxt
                    yield
                assert cur == 0
                prog["prep"] = (i, m + 1)

        def chain(i):
            p = i % 2
            AR_, gC_, bon, sil = AR[p], gC[p], bonb[p], silb[p]
            pc_ = ps_o
            for m in range(4):
                while prog["prep"] < (i, m + 1):
                    yield "blocked"
                q = m % 2
                SA_, SB_, TR_, TTf = SA[q], SBs[q], TRs[q], NLT[q][0]
                for cc, c in enumerate([2 * m, 2 * m + 1]):
                    for h in range(2):
                        hs = slice(h * 64, (h + 1) * 64)
                        mm(pc_[hs, 128:192], AR_[hs, c, 0, :], Sst[hs, :], [AR_, Sst], [pc_], start=True, stop=False, sig=False)
                        mm(pc_[hs, 128:192], SB_[hs, cc, 0:64], TR_[hs, cc, 0:64], [SB_, TR_], [pc_], start=False, stop=True, sig=(h == 1))
                    cp("scalar", Ybf[:], pc_[:, 128:192], [pc_], [Ybf])
                    yield
                    for h in range(2):
                        hs = slice(h * 64, (h + 1) * 64)
                        mm(pc_[hs, 192:256], TTf[hs, cc, 128:192], Ybf[hs, :], [TTf, Ybf], [pc_], sig=(h == 1))
                    cp("vector", Ubf[:], pc_[:, 192:256], [pc_], [Ubf])
                    yield
                    for h in range(2):
                        hs = slice(h * 64, (h + 1) * 64)
                        oo = pc_[hs, cc * 64:(cc + 1) * 64]
                        mm(oo, Sst[hs, :], AR_[hs, c, 1, :], [Sst, AR_], [pc_], start=True, stop=False, sig=False)
                        mm(oo, Ubf[hs, :], SA_[hs, cc, 128:192], [Ubf, SA_], [pc_], start=False, stop=False, sig=False)
                        mm(oo, TR_[hs, cc, 0:64], SB_[hs, cc, 64:128], [TR_, SB_], [pc_], start=False, stop=True, sig=False)
                        mm(pc_[hs, 128:192], TR_[hs, cc, 64:128], Ubf[hs, :], [TR_, Ubf], [pc_], start=True, stop=False, sig=False)
                        mm(pc_[hs, 128:192], TR_[hs, cc, 128:192], TR_[hs, cc, 0:64], [TR_], [pc_], start=False, stop=True, sig=(h == 1))
                    stt("vector", Sm[:], Sm[:], gC_[:, c:c + 1], pc_[:, 128:192], ALU.mult, ALU.add, [Sm, gC_, pc_], [Sm])
                    cp("vector", Sst[:], Sm[:], [Sm], [Sst])
                    if cc == 1:
                        cp("vector", Ot[:, m * 128:(m + 1) * 128], pc_[:, 0:128], [pc_], [Ot])
                        prog["chain_g"] = 4 * i + m + 1
                    yield
            stat = ps_sc[0]
            mm(stat[:, :], onesbd_f, Ot[:], [cst_t, Ot], [stat])
            xc = fx[0]
            stt("vector", xc[:], stat[:, :], -1.0 / 64, Ot[:], ALU.mult, ALU.add, [stat, Ot], [xc])
            xq = Ot
            act(xq[:], xc[:], AF.Square, [xc], [xq])
            yield
            mm(stat[:, :], onesbd_f, xq[:], [cst_t, xq], [stat])
            rv = fx[1]
            act(rv[:], stat[:, :], AF.Sqrt, [stat, dp], [rv], bias=dp[:, 4:5], scale=1.0 / 64)
            recip("vector", rv[:], rv[:], [rv], [rv])
            tt("vector", xc[:], xc[:], rv[:], ALU.mult, [xc, rv], [xc])
            yield
            ts("vector", xc[:], xc[:], pp_t[:, 10:11], pp_t[:, 11:12], ALU.mult, ALU.add, [xc, pp_t], [xc])
            tt("gpsimd", xc[:], xc[:], bon[:], ALU.add, [xc, bon], [xc])
            tt("vector", yT[0][:], xc[:], sil[:], ALU.mult, [xc, sil], [yT[0]])
            yield

        def attn(i):
            p = i % 2
            q_, kv_, ga_ = zsA[p]
            act(tb[2][:], q_[:], AF.Square, [q_], [tb[2]])
            mm(gA[:, :], onesbd_b[:], tb[2][:], [onesbd_b, tb[2]], [gA])
            rq = gx[0]
            act(rq[:], gA[:, :], AF.Sqrt, [gA, dp], [rq], bias=dp[:, 3:4], scale=1.0 / 64)
            recip("vector", rq[:], rq[:], [rq], [rq])
            stt("vector", qn[:], q_[:], dp[:, 1:2], rq[:], ALU.mult, ALU.mult, [q_, dp, rq], [qn])
            yield
            act(tb[3][0:64, :], kv_[0:64, :], AF.Square, [kv_], [tb[3]])
            mm(gB[0:64, :], onesbd_b[0:64, 0:64], tb[3][0:64, :], [onesbd_b, tb[3]], [gB])
            rk = gx[1]
            act(rk[0:64, :], gB[0:64, :], AF.Sqrt, [gB, dp], [rk], bias=dp[0:64, 3:4], scale=1.0 / 64)
            recip("vector", rk[0:64, :], rk[0:64, :], [rk], [rk])
            cp("scalar", tb[3][:, :], kv_[:, :], [kv_, tb[3]], [tb[3]])
            yield
            for n in range(4):
                gb = 4 * i + n
                sl = gb % NSL
                ts_ = slice(n * 128, (n + 1) * 128)
                stt("vector", KN[sl][0:64, :], kv_[0:64, ts_], pp_t[0:64, 13:14], rk[0:64, ts_], ALU.mult, ALU.mult, [kv_, pp_t, rk], [KN[sl]])
                cp("vector", KN[sl][64:128, :], KN[sl][0:64, :], [KN[sl]], [KN[sl]])
                mm(gB[:, 256:320], tb[3][64:128, ts_], identb[64:128, 64:128], [tb[3], identb], [gB])
                cp("scalar", VA[sl][:, 0:64], gB[:, 256:320], [gB], [VA[sl]])
                cp("gpsimd", VB[sl][:, 64:128], VA[sl][:, 0:64], [VA[sl]], [VB[sl]])
                yield
            for n in range(4):
                gb = 4 * i + n
                sl = gb % NSL
                slp = (gb - 1) % NSL
                ts_ = slice(n * 128, (n + 1) * 128)
                pcs = [1] if gb == 0 else [0, 1]
                for h in range(2):
                    hs = slice(h * 64, (h + 1) * 64)
                    lp = gA if h == 0 else gB
                    for q_i, pc in enumerate(pcs):
                        ksrc = KN[slp] if pc == 0 else KN[sl]
                        mm(lp[:, pc * 128:(pc + 1) * 128], ksrc[hs, :], qn[hs, ts_], [ksrc, qn], [lp], sig=(q_i == len(pcs) - 1))
                for h in range(2):
                    lp = gA if h == 0 else gB
                    lo = 128 if gb == 0 else 0
                    act(Es[:, h * 256 + lo:(h + 1) * 256], lp[:, lo:256], AF.Exp, [lp], [Es])
                    tt("vector", Em[:, h * 256 + lo:(h + 1) * 256], Es[:, h * 256 + lo:(h + 1) * 256], EB[:, h * 256 + lo:(h + 1) * 256], ALU.mult, [Es, EB], [Em])
                yield
                for h in range(2):
                    Vx = VA if h == 0 else VB
                    ob = gA[:, 256 + h * 128:256 + (h + 1) * 128]
                    for q_i, pc in enumerate(pcs):
                        vsrc = Vx[slp] if pc == 0 else Vx[sl]
                        mm(ob, vsrc[:, :], Em[:, (h * 2 + pc) * 128:(h * 2 + pc + 1) * 128], [vsrc, Em], [gA], start=(q_i == 0), stop=(q_i == len(pcs) - 1), sig=(q_i == len(pcs) - 1))
                cp("scalar", pvs[:, n, :], gA[:, 256:512], [gA], [pvs])
                yield
            rec = gx[2]
            r3 = rec[:].rearrange("p (n t) -> p n t", t=128)
            cp("vector", r3[0:64, :, :], pvs[64:128, :, 0:128], [pvs], [rec])
            cp("vector", r3[64:128, :, :], pvs[0:64, :, 128:256], [pvs], [rec])
            ts("vector", rec[:], rec[:], dp[:, 2:3], None, ALU.add, None, [rec, dp], [rec])
            recip("vector", rec[:], rec[:], [rec], [rec])
            yield
            yat = gx[3]
            y3 = yat[:].rearrange("p (n t) -> p n t", t=128)
            tt("vector", y3[0:64, :, :], pvs[0:64, :, 0:128], r3[0:64, :, :], ALU.mult, [pvs, rec], [yat])
            tt("vector", y3[64:128, :, :], pvs[64:128, :, 128:256], r3[64:128, :, :], ALU.mult, [pvs, rec], [yat])
            sga = gx[0]
            act(sga[:], ga_[:], AF.Silu, [ga_], [sga])
            tt("gpsimd", yT[1][:], yat[:], sga[:], ALU.mult, [yat, sga], [yT[1]])
            yield

        def run_streams(gens):
            live = list(gens)
            weight = {"fe": int(os.environ.get("KWFE", "3")), "prep": 1, "chain": 1, "attn": 1}
            while live:
                progressed = False
                for g in list(live):
                    for _ in range(weight.get(g.__name__, 1)):
                        try:
                            P.tag = g.__name__
                            r = next(g)
                            if r != "blocked":
                                progressed = True
                            else:
                                break
                        except StopIteration:
                            live.remove(g)
                            progressed = True
                            break
                assert progressed, "scheduler deadlock"

        def emit_out(i):
            t0 = i * TT_
            tk0 = P.dma(lambda e, t0=t0: e.dma_start(out=a_in.ap()[0:128, t0:t0 + TT_], in_=yT[0][:]), reads=[yT[0]], writes=[bin_])
            tk1 = P.dma(lambda e, t0=t0: e.dma_start(out=a_in.ap()[128:256, t0:t0 + TT_], in_=yT[1][:]), reads=[yT[1]], writes=[bin_])
            if not fused:
                out_tickets.extend([tk0, tk1])

        run_streams([fe(0)])
        for i in range(NT):
            en = os.environ.get("KSTREAMS", "prep,chain,attn").split(",")
            gens = [g for nme, g in (("prep", prep(i)), ("chain", chain(i)), ("attn", attn(i))) if nme in en]
            if i + 1 < NT:
                gens.append(fe(i + 1))
            run_streams(gens)
            emit_out(i)

        if fused:
            P.op("gpsimd", lambda e: e.collective_compute("AllGather", ALU.bypass, replica_groups=[list(range(NCORE))], ins=[a_in.ap().opt()], outs=[a_out.ap().opt()]), [bin_], [bout])
            for i in range(NT):
                t0 = i * TT_
                for r in range(NCORE):
                    P.dma(lambda e, t0=t0, r=r: e.dma_start(out=ya[:, 2 * r:2 * r + 2, :], in_=a_out.ap()[r * 256:(r + 1) * 256, t0:t0 + TT_].rearrange("(h p) t -> p h t", p=128)), reads=[bout], writes=[ya])
                for dc in range(2):
                    P.dma(lambda e, dc=dc, t0=t0: e.dma_start(out=xr[dc][:], in_=xres[dc * 128:(dc + 1) * 128, t0:t0 + TT_]), writes=[xr[dc]])
                    pacc = ps_in[dc]
                    for j in range(KC):
                        mm(pacc[:, :], Wo[:, j, dc * 128:(dc + 1) * 128], ya[:, j, :], [Wo, ya], [pacc], start=(j == 0), stop=(j == KC - 1), sig=(j == KC - 1))
                    tt("vector", ost[dc][:], pacc[:, :], xr[dc][:], ALU.add, [pacc, xr[dc]], [ost[dc]])
                    out_tickets.append(P.dma(lambda e, dc=dc, t0=t0: e.dma_start(out=outT[dc * 128:(dc + 1) * 128, t0:t0 + TT_], in_=ost[dc][:]), reads=[ost[dc]], eng="scalar"))
        P.wait_all("sync", out_tickets)
        P.emit()
    return nc


def build_out(S):
    NT = S // TT_
    nc = bass.Bass("TRN2", target_bir_lowering=False)
    din = lambda n, sh, dt=F32: nc.dram_tensor(n, sh, dt, kind="ExternalInput").ap()
    yall = din("yall", [D, S], BF16)
    xres = din("xres", [256, S])
    woc = din("woc", [D, 256])
    outT = nc.dram_tensor("outT", [256, S], F32, kind="ExternalOutput").ap()
    es = ExitStack()
    with es:
        P = Prog(nc, es)
        cnt = [0]

        def sb(shape, dt=F32):
            cnt[0] += 1
            return T(es.enter_context(nc.sbuf_tensor("sb%d" % cnt[0], shape, dt)))

        def psb(shape, dt=F32):
            cnt[0] += 1
            return T(es.enter_context(nc.psum_tensor("ps%d" % cnt[0], shape, dt)))

        Wo = sb([128, KC, 256], BF16)
        wst = [sb([128, 256]) for _ in range(2)]
        for kc in range(KC):
            s_ = wst[kc % 2]
            P.dma(lambda e, kc=kc, s_=s_: e.dma_start(out=s_[:], in_=woc[kc * 128:(kc + 1) * 128, :]), writes=[s_])
            P.op("vector", lambda e, kc=kc, s_=s_: e.tensor_copy(out=Wo[:, kc, :], in_=s_[:]), [s_], [Wo])
        ya = [sb([128, KC, TT_], BF16) for _ in range(2)]
        xr = [[sb([128, TT_]) for _ in range(2)] for _ in range(2)]
        ost = [[sb([128, TT_]) for _ in range(2)] for _ in range(2)]
        pacc = [psb([128, 512]) for _ in range(4)]
        tickets = []
        for i in range(NT):
            t0 = i * TT_
            yb = ya[i % 2]
            P.dma(lambda e, t0=t0, yb=yb: e.dma_start(out=yb[:], in_=yall[:, t0:t0 + TT_].rearrange("(j p) t -> p j t", p=128)), writes=[yb])
            for dc in range(2):
                xb_ = xr[i % 2][dc]
                ob_ = ost[i % 2][dc]
                pc_ = pacc[(i % 2) * 2 + dc]
                P.dma(lambda e, t0=t0, dc=dc, xb_=xb_: e.dma_start(out=xb_[:], in_=xres[dc * 128:(dc + 1) * 128, t0:t0 + TT_]), writes=[xb_])
                for j in range(KC):
                    P.op("tensor", lambda e, j=j, dc=dc, yb=yb, pc_=pc_: e.matmul(pc_[:, :], lhsT=Wo[:, j, dc * 128:(dc + 1) * 128], rhs=yb[:, j, :], start=(j == 0), stop=(j == KC - 1)), [Wo, yb], [pc_], sig=(j == KC - 1))
                P.op("vector", lambda e, pc_=pc_, xb_=xb_, ob_=ob_: e.tensor_tensor(out=ob_[:], in0=pc_[:, :], in1=xb_[:], op=ALU.add), [pc_, xb_], [ob_])
                tickets.append(P.dma(lambda e, t0=t0, dc=dc, ob_=ob_: e.dma_start(out=outT[dc * 128:(dc + 1) * 128, t0:t0 + TT_], in_=ob_[:]), reads=[ob_], eng="scalar"))
        P.wait_all("sync", tickets)
        P.emit()
    return nc


D_RWKV = 1024
RWKV_COLS = 4 * 1024 + 128
D_ATT = 1024


def _t5_bucket(dist):
    n = np.maximum(dist, 0)
    nf = np.maximum(n, 1).astype(np.float32)
    large = 16 + (np.log(nf / 16) / math.log(128 / 16) * 16).astype(np.int32)
    large = np.minimum(large, 31)
    return np.where(n < 16, n, large)


def _consts():
    c = np.zeros((128, 1728), np.float32)
    c[:, 0:128] = np.eye(128, dtype=np.float32)
    c[0:64, 128:192] = 1.0
    c[64:128, 192:256] = 1.0
    row = (np.arange(128) % 64)[:, None]
    col = np.arange(64)[None, :]
    lower = (col < row).astype(np.float32)
    upper = (col > row).astype(np.float32)
    upper_i = (col >= row).astype(np.float32)
    for cc in range(2):
        c[:, 256 + cc * 192 + 0:256 + cc * 192 + 64] = lower
        c[:, 256 + cc * 192 + 64:256 + cc * 192 + 128] = upper
        c[:, 256 + cc * 192 + 128:256 + cc * 192 + 192] = upper_i
        c[:, 640 + cc * 128 + 0:640 + cc * 128 + 64] = upper
        c[:, 640 + cc * 128 + 64:640 + cc * 128 + 128] = upper_i
    c[:, 896:960] = (col == row).astype(np.float32)
    rm = np.ones(512, np.float32)
    rm[0::64] = 0.0
    c[:, 960:1472] = rm[None, :]
    kj = np.arange(128)[:, None]
    qi = np.arange(128)[None, :]
    c[:, 1472:1600] = (qi < kj).astype(np.float32)
    c[:, 1600:1728] = (kj <= qi).astype(np.float32)
    return c


def _bias_T(rel_bias, heads):
    out = np.zeros((128, 512), np.float32)
    kj = np.arange(128)[:, None]
    qi = np.arange(128)[None, :]
    for hi, hd in enumerate(heads):
        for pc in range(2):
            dist = (128 + qi - kj) if pc == 0 else (qi - kj)
            b = _t5_bucket(dist)
            out[:, (hi * 2 + pc) * 128:(hi * 2 + pc + 1) * 128] = rel_bias[b, hd]
    return out


def _prep_inputs(S, x, norm_w, w_in, w_out, mu_rwkv, w0, w2, a0, a2, k_k, k_a, r_k,
                 lnx_w, lnx_b, q_norm_w, k_norm_w, sinks, rel_bias):
    f = lambda a: np.asarray(a, dtype=np.float32)
    x, norm_w, w_in, w_out = f(x)[0], f(norm_w)[0], f(w_in)[0], f(w_out)[0]
    mu, w0, w2, a0, a2 = f(mu_rwkv)[0], f(w0)[0], f(w2)[0], f(a0)[0], f(a2)[0]
    k_k, k_a, r_k = f(k_k)[0], f(k_a)[0], f(r_k)[0].reshape(-1)
    lnx_w, lnx_b, qw, kw, sinks, rel_bias = f(lnx_w)[0], f(lnx_b)[0], f(q_norm_w)[0], f(k_norm_w)[0], f(sinks)[0], f(rel_bias)
    xT = np.ascontiguousarray(x.T)
    cst = _consts()
    maps = []
    for c in range(NCORE):
        hsl = slice(c * 128, (c + 1) * 128)
        kvh = c // 4
        A0 = RWKV_COLS
        cols = np.concatenate([
            np.arange(c * 128, (c + 1) * 128),
            D_RWKV + np.arange(c * 128, (c + 1) * 128),
            2 * D_RWKV + np.arange(c * 128, (c + 1) * 128),
            3 * D_RWKV + np.arange(c * 128, (c + 1) * 128),
            4 * D_RWKV + np.arange(0, 128),
            A0 + np.arange(c * 128, (c + 1) * 128),
            A0 + D_ATT + kvh * 64 + np.arange(64),
            A0 + D_ATT + 128 + kvh * 64 + np.arange(64),
            A0 + D_ATT + 256 + np.arange(c * 128, (c + 1) * 128),
        ])
        wc = np.ascontiguousarray(w_in[:, cols])
        rows = np.concatenate([np.concatenate([r * 128 + np.arange(128), 1024 + r * 128 + np.arange(128)]) for r in range(NCORE)])
        woc = np.ascontiguousarray(w_out[rows][:, c * 256:(c + 1) * 256])
        pp = np.zeros((128, 32), np.float32)
        for j in range(5):
            pp[:, j] = mu[cols[j * 128:(j + 1) * 128]]
        pp[:, 5] = w0[hsl]
        pp[:, 6] = a0[hsl]
        pp[:, 7] = k_k[hsl]
        pp[:, 8] = k_a[hsl]
        pp[:, 9] = r_k[hsl]
        pp[:, 10] = lnx_w[hsl]
        pp[:, 11] = lnx_b[hsl]
        pp[:, 12] = np.concatenate([qw, qw])
        pp[:, 13] = np.concatenate([kw, kw])
        pp[0:64, 14] = sinks[2 * c]
        pp[64:128, 14] = sinks[2 * c + 1]
        pp[0:64, 15] = sinks[2 * c + 1]
        pp[64:128, 15] = sinks[2 * c]
        pp[:, 16:32] = norm_w.reshape(16, 128).T
        w2a2 = np.concatenate([w2[:, hsl], a2[:, hsl]], axis=0)
        maps.append({
            "xT": xT,
            "xres": np.ascontiguousarray(xT[c * 256:(c + 1) * 256, :]),
            "wc": wc, "woc": woc, "w2a2": np.ascontiguousarray(w2a2), "pp": pp, "cst": cst,
            "abias": _bias_T(rel_bias, [2 * c, 2 * c + 1]),
        })
    return maps


FUSED = bool(int(os.environ.get("KFUSED", "0")))


def kernel(**inputs):
    S = inputs["x"].shape[1]
    maps = _prep_inputs(S, **inputs)
    if FUSED:
        nc = build(S, fused=True)
        res = run_bass_kernel_spmd(nc, maps, core_ids=list(range(NCORE)))
        if int(os.environ.get('KDBG', '0')):
            global DBG_Y
            DBG_Y = [np.asarray(res.results[c]["dbg_y"]).astype(np.float32) for c in range(NCORE)]
    else:
        m1 = [{k: v for k, v in m.items() if k not in ("xres", "woc")} for m in maps]
        nc = build(S, fused=False)
        res = run_bass_kernel_spmd(nc, m1, core_ids=list(range(NCORE)))
        yall = np.concatenate([np.asarray(res.results[c]["yc"]) for c in range(NCORE)], axis=0)
        m2 = [{"yall": yall, "xres": m["xres"], "woc": m["woc"]} for m in maps]
        nc2 = build_out(S)
        res = run_bass_kernel_spmd(nc2, m2, core_ids=list(range(NCORE)))
    outT = np.concatenate([res.results[c]["outT"] for c in range(NCORE)], axis=0)
    return np.ascontiguousarray(outT.T)[None].astype(np.float32)
```
